# Optimizing a Trainium2 kernel written in Bass

```python
import jax, jax.numpy as jnp
from jax import lax
import numpy as np

D_MODEL = 1024
BATCH = 32
SEQ = 2048
DEPTH = 1

HEAD_DIM = 64
N_Q_HEADS = 16
N_KV_HEADS = 2
GQA_GROUP = N_Q_HEADS // N_KV_HEADS
WINDOW = 128
BLOCK = 128
ROPE_THETA = 10000.0
CONV_CH = D_MODEL
CONV_WIDTH = 31
D_FF = 2816
FFN_CONV_WIDTH = 3
RMS_EPS = 1e-6
LN_EPS = 1e-5

Q_W = N_Q_HEADS * HEAD_DIM
KV_W = N_KV_HEADS * HEAD_DIM
IN_SPLITS = (2 * CONV_CH, Q_W, KV_W, KV_W, D_MODEL, D_MODEL)
IN_WIDTH = sum(IN_SPLITS)

kernel_name = "hybrid_conv_swa_sink_convffn_block"


def rms_norm(x, g):
    xf = x.astype(jnp.float32)
    y = xf * lax.rsqrt(jnp.mean(xf * xf, axis=-1, keepdims=True) + RMS_EPS)
    return (y * g.astype(jnp.float32)).astype(x.dtype)


def layer_norm(x, g, b):
    xf = x.astype(jnp.float32)
    mu = jnp.mean(xf, axis=-1, keepdims=True)
    xc = xf - mu
    var = jnp.mean(xc * xc, axis=-1, keepdims=True)
    y = xc * lax.rsqrt(var + LN_EPS) * g.astype(jnp.float32) + b.astype(jnp.float32)
    return y.astype(x.dtype)


def causal_depthwise_conv(x, w, b):
    k = w.shape[0]
    c = x.shape[-1]
    y = lax.conv_general_dilated(
        x, w[:, None, :].astype(x.dtype), window_strides=(1,), padding=((k - 1, 0),),
        dimension_numbers=("NWC", "WIO", "NWC"), feature_group_count=c)
    return y + b.astype(x.dtype)


def rope_tables(positions):
    inv_freq = ROPE_THETA ** (-jnp.arange(0, HEAD_DIM, 2, dtype=jnp.float32) / HEAD_DIM)
    ang = positions.astype(jnp.float32)[..., None] * inv_freq
    return jnp.cos(ang)[:, :, None, :], jnp.sin(ang)[:, :, None, :]


def apply_rope(x, cos, sin):
    xf = x.astype(jnp.float32)
    x1, x2 = jnp.split(xf, 2, axis=-1)
    return jnp.concatenate([x1 * cos - x2 * sin, x2 * cos + x1 * sin], axis=-1).astype(x.dtype)


def sliding_window_sink_attention(q, k, v, sinks):
    b, s = q.shape[0], q.shape[1]
    nb = s // BLOCK
    qb = q.reshape(b, nb, BLOCK, N_KV_HEADS, GQA_GROUP, HEAD_DIM)

    def band(t):
        prev = jnp.pad(t, ((0, 0), (BLOCK, 0), (0, 0), (0, 0)))[:, :s]
        prev = prev.reshape(b, nb, BLOCK, N_KV_HEADS, HEAD_DIM)
        cur = t.reshape(b, nb, BLOCK, N_KV_HEADS, HEAD_DIM)
        return jnp.concatenate([prev, cur], axis=2)

    kb, vb = band(k), band(v)
    scores = jnp.einsum("bnqhgd,bnkhd->bnhgqk", qb, kb).astype(jnp.float32) * (HEAD_DIM ** -0.5)
    qi = jnp.arange(BLOCK)[:, None]
    kj = jnp.arange(2 * BLOCK)[None, :]
    rel = qi + BLOCK - kj
    in_window = (rel >= 0) & (rel < WINDOW)
    key_abs = jnp.arange(nb)[:, None, None] * BLOCK - BLOCK + kj[None]
    mask = in_window[None] & (key_abs >= 0)
    scores = jnp.where(mask[None, :, None, None], scores, -jnp.inf)
    sink = sinks.astype(jnp.float32).reshape(1, 1, N_KV_HEADS, GQA_GROUP, 1, 1)
    m = jnp.maximum(jnp.max(scores, axis=-1, keepdims=True), sink)
    p = jnp.exp(scores - m)
    probs = p / (jnp.sum(p, axis=-1, keepdims=True) + jnp.exp(sink - m))
    out = jnp.einsum("bnhgqk,bnkhd->bnqhgd", probs.astype(v.dtype), vb)
    return out.reshape(b, s, N_Q_HEADS * HEAD_DIM)


def setup_inputs(seed: int = 0) -> dict:
    key = jax.random.key(seed)
    ks = jax.random.split(key, 24)
    L, D = DEPTH, D_MODEL

    def nrm(k, shape, scale):
        return jax.random.normal(k, shape, dtype=jnp.float32) * scale

    def gain(k, n):
        return 1.0 + nrm(k, (L, n), 0.02)

    x = jax.random.normal(ks[0], (BATCH, SEQ, D), dtype=jnp.float32)
    start = jax.random.randint(ks[1], (BATCH, 1), 0, 1024, dtype=jnp.int32)
    positions = start + jnp.arange(SEQ, dtype=jnp.int32)[None, :]
    return {
        "x": x,
        "positions": positions,
        "ln_mix_pre": gain(ks[2], D),
        "w_in": nrm(ks[3], (L, D, IN_WIDTH), D ** -0.5),
        "b_gate": nrm(ks[4], (L, 2 * D), 0.02),
        "conv_dw_w": nrm(ks[5], (L, CONV_WIDTH, CONV_CH), CONV_WIDTH ** -0.5),
        "conv_dw_b": nrm(ks[6], (L, CONV_CH), 0.02),
        "conv_ln_g": gain(ks[7], CONV_CH),
        "conv_ln_b": nrm(ks[8], (L, CONV_CH), 0.02),
        "w_conv_out": nrm(ks[9], (L, CONV_CH, D), CONV_CH ** -0.5),
        "attn_sinks": nrm(ks[10], (L, N_Q_HEADS), 0.5),
        "w_attn_out": nrm(ks[11], (L, Q_W, D), Q_W ** -0.5),
        "w_out": nrm(ks[12], (L, D, D), D ** -0.5),
        "ln_mix_post": gain(ks[13], D),
        "ln_ffn_pre": gain(ks[14], D),
        "w_up": nrm(ks[15], (L, D, 2 * D_FF), D ** -0.5),
        "ffn_dw_w": nrm(ks[16], (L, FFN_CONV_WIDTH, 2 * D_FF), FFN_CONV_WIDTH ** -0.5),
        "ffn_dw_b": nrm(ks[17], (L, 2 * D_FF), 0.02),
        "w_down": nrm(ks[18], (L, D_FF, D), D_FF ** -0.5),
        "ln_ffn_post": gain(ks[19], D),
    }


def reference(x, positions, ln_mix_pre, w_in, b_gate, conv_dw_w, conv_dw_b, conv_ln_g, conv_ln_b,
              w_conv_out, attn_sinks, w_attn_out, w_out, ln_mix_post, ln_ffn_pre, w_up,
              ffn_dw_w, ffn_dw_b, w_down, ln_ffn_post):
    b, s, _ = x.shape
    cos, sin = rope_tables(positions)
    split_idx = list(np.cumsum(IN_SPLITS)[:-1])
    for l in range(DEPTH):
        h = rms_norm(x, ln_mix_pre[l])
        proj = h @ w_in[l]
        conv_in, q, k, v, g_conv_logit, g_attn_logit = jnp.split(proj, split_idx, axis=-1)
        gates = jax.nn.sigmoid(jnp.concatenate([g_conv_logit, g_attn_logit], axis=-1) + b_gate[l])
        g_conv, g_attn = jnp.split(gates, 2, axis=-1)

        a_val, a_gate = jnp.split(conv_in, 2, axis=-1)
        u = a_val * jax.nn.sigmoid(a_gate)
        u = causal_depthwise_conv(u, conv_dw_w[l], conv_dw_b[l])
        u = jax.nn.silu(layer_norm(u, conv_ln_g[l], conv_ln_b[l]))
        y_conv = u @ w_conv_out[l]

        q = apply_rope(q.reshape(b, s, N_Q_HEADS, HEAD_DIM), cos, sin)
        k = apply_rope(k.reshape(b, s, N_KV_HEADS, HEAD_DIM), cos, sin)
        v = v.reshape(b, s, N_KV_HEADS, HEAD_DIM)
        y_attn = sliding_window_sink_attention(q, k, v, attn_sinks[l]) @ w_attn_out[l]

        merged = g_conv * y_conv + g_attn * y_attn
        x = x + rms_norm(merged @ w_out[l], ln_mix_post[l])

        h = rms_norm(x, ln_ffn_pre[l])
        up = causal_depthwise_conv(h @ w_up[l], ffn_dw_w[l], ffn_dw_b[l])
        f_gate, f_val = jnp.split(up, 2, axis=-1)
        z = jax.nn.silu(f_gate) * f_val
        x = x + rms_norm(z @ w_down[l], ln_ffn_post[l])
    return x
```

```python
import contextlib
import numpy as np
import concourse.bass as bass
import concourse.mybir as mybir
from concourse.bass_utils import run_bass_kernel_spmd

F32 = mybir.dt.float32
BF16 = mybir.dt.bfloat16
I32 = mybir.dt.int32
AF = mybir.ActivationFunctionType
ALU = mybir.AluOpType

ENGS = ["pe", "act", "dve", "pool", "sp"]
NCORES = 8
NT = 16
NWSLOT = 4
PI = float(np.pi)


class Prog:
    def __init__(self, nc, n_dma_sems=12, strict=("act", "dve", "pool")):
        self.nc = nc
        self.ops = {e: [] for e in ENGS}
        self.lastw = {}
        self.readers = {}
        self.seen = {e: {} for e in ENGS}
        self.seen_dma = {e: set() for e in ENGS}
        self.strict = set(strict)
        self.n_dma_sems = n_dma_sems
        self.ndma = {e: 0 for e in ENGS}
        self.regions = {}
        self.conf = {}

    def region(self, key, lo, hi):
        self.regions[key] = (lo, hi)
        self.conf = {}

    def _conf(self, k):
        c = self.conf.get(k)
        if c is None:
            if k in self.regions:
                lo, hi = self.regions[k]
                c = [k2 for k2, (l2, h2) in self.regions.items() if l2 < hi and lo < h2]
            else:
                c = [k]
            self.conf[k] = c
        return c

    def op(self, eng, fn, reads=(), writes=(), dma=False, extra=()):
        idx = len(self.ops[eng])
        rec = dict(eng=eng, fn=fn, waits={}, dma_waits=[], signal=False, dma=dma, idx=idx)
        deps = set(extra)
        for k0 in reads:
            for k in self._conf(k0):
                w = self.lastw.get(k)
                if w is not None:
                    deps.add(w)
                if k.startswith("ps"):
                    deps.update(r for r in self.readers.get(k, ()) if r[0] != eng)
        for k0 in writes:
            for k in self._conf(k0):
                w = self.lastw.get(k)
                if w is not None:
                    deps.add(w)
                deps.update(self.readers.get(k, ()))
        for (e2, i2) in deps:
            r2 = self.ops[e2][i2]
            if r2["dma"]:
                if (e2, i2) in self.seen_dma[eng]:
                    continue
                self.seen_dma[eng].add((e2, i2))
                rec["dma_waits"].append((e2, i2))
            else:
                if e2 == eng and eng not in self.strict:
                    continue
                if i2 <= self.seen[eng].get(e2, -1):
                    continue
                rec["waits"][e2] = max(rec["waits"].get(e2, -1), i2)
        for e2, i2 in rec["waits"].items():
            self.seen[eng][e2] = i2
            self.ops[e2][i2]["signal"] = True
        if dma:
            rec["dma_j"] = self.ndma[eng]
            self.ndma[eng] += 1
        self.ops[eng].append(rec)
        for k in writes:
            self.lastw[k] = (eng, idx)
            self.readers[k] = []
        for k in reads:
            self.readers.setdefault(k, []).append((eng, idx))
        return (eng, idx)

    def fence(self, extra=()):
        last = []
        for e in ("pe", "act", "dve", "pool"):
            for i in range(len(self.ops[e]) - 1, -1, -1):
                r = self.ops[e][i]
                if not r["dma"] and r["fn"] is not None:
                    last.append((e, i))
                    break
        for e in ENGS:
            self.op(e, None, extra=[d for d in last if d[0] != e] + list(extra))

    def emit(self, final_waits=()):
        nc = self.nc
        with contextlib.ExitStack() as st:
            csem = {e: st.enter_context(nc.semaphore("c_" + e)) for e in ENGS}
            dsem = {e: [st.enter_context(nc.semaphore("d_%s_%d" % (e, i))) for i in range(self.n_dma_sems)]
                    for e in ENGS if self.ndma[e] > 0}
            for e in ENGS:
                c = 0
                for r in self.ops[e]:
                    if r["signal"] and not r["dma"]:
                        c += 1
                    r["cval"] = c
            N = self.n_dma_sems

            def dma_sem_val(e2, i2):
                j = self.ops[e2][i2]["dma_j"]
                return e2, j % N, 16 * (j // N + 1)

            block = st.enter_context(nc.Block())

            def run(e, eng):
                for r in self.ops[e]:
                    for e2, i2 in r["waits"].items():
                        eng.wait_ge(csem[e2], self.ops[e2][i2]["cval"])
                    dw = {}
                    for (e2, i2) in r["dma_waits"]:
                        q, s, v = dma_sem_val(e2, i2)
                        dw[(q, s)] = max(dw.get((q, s), 0), v)
                    for (q, s), v in dw.items():
                        eng.wait_ge(dsem[q][s], v)
                    if r["fn"] is None:
                        continue
                    if r["dma"]:
                        j = r["dma_j"]
                        if j >= N:
                            eng.wait_ge(dsem[e][j % N], 16 * (j // N))
                    ins = r["fn"](eng)
                    if r["dma"]:
                        ins.then_inc(dsem[e][r["dma_j"] % N], 16)
                    elif r["signal"]:
                        ins.then_inc(csem[e], 1)
                if e == "sp":
                    dw = {}
                    for (e2, i2) in final_waits:
                        q, s, v = dma_sem_val(e2, i2)
                        dw[(q, s)] = max(dw.get((q, s), 0), v)
                    for (q, s), v in dw.items():
                        eng.wait_ge(dsem[q][s], v)

            @block.tensor
            def _(eng):
                run("pe", eng)

            @block.scalar
            def _(eng):
                run("act", eng)

            @block.vector
            def _(eng):
                run("dve", eng)

            @block.gpsimd
            def _(eng):
                run("pool", eng)

            @block.sync
            def _(eng):
                run("sp", eng)


O_G1, O_G3, O_BG, O_CW, O_CB, O_LG, O_LB, O_FW, O_FB, O_IF, O_SG = 0, 8, 16, 32, 280, 288, 296, 304, 436, 480, 481
NPRM = 482
NROW = 2064


def build_program():
    nc = bass.Bass("TRN2", target_bir_lowering=False)
    x_d = nc.dram_tensor("x", [8192, 1024], F32, kind="ExternalInput").ap()
    pos_d = nc.dram_tensor("pos", [4, 2048], I32, kind="ExternalInput").ap()
    win_d = nc.dram_tensor("w_in_p", [1024, 5376], F32, kind="ExternalInput").ap()
    wb_d = nc.dram_tensor("w_b_p", [1024, 2048], F32, kind="ExternalInput").ap()
    wout_d = nc.dram_tensor("w_out", [1024, 1024], F32, kind="ExternalInput").ap()
    wup_d = nc.dram_tensor("w_up_p", [1024, 5632], F32, kind="ExternalInput").ap()
    wdn_d = nc.dram_tensor("w_down", [2816, 1024], F32, kind="ExternalInput").ap()
    prm_d = nc.dram_tensor("prm", [128, NPRM], F32, kind="ExternalInput").ap()
    rowp_d = nc.dram_tensor("rowp", [1, NROW], F32, kind="ExternalInput").ap()
    y_d = nc.dram_tensor("y", [8192, 1024], F32, kind="ExternalOutput").ap()
    wA = nc.dram_tensor("wA", [11, 128, 8, 512], BF16).ap()
    wB = nc.dram_tensor("wB", [4, 128, 8, 512], BF16).ap()
    wC = nc.dram_tensor("wC", [2, 128, 8, 512], BF16).ap()
    wD = nc.dram_tensor("wD", [11, 128, 8, 512], BF16).ap()
    wE = nc.dram_tensor("wE", [2, 128, 22, 512], BF16).ap()
    wF = nc.dram_tensor("wF", [8, 128, 4096], BF16).ap()
    wG = nc.dram_tensor("wG", [5, 128, 4096], BF16).ap()

    def sb(name, shape, dt):
        return nc.alloc_sbuf_tensor(name, shape, dt).ap()

    xbuf = sb("xbuf", [128, 2, 4, 1024], F32)
    wring = sb("wring", [128, NWSLOT, 8, 512], BF16)
    wflat = wring.rearrange("p s k t -> p s (k t)")
    hTm = sb("hTm", [128, 4, 1024], BF16)
    hTa = sb("hTa", [128, 8, 512], BF16)
    hTb = sb("hTb", [128, 8, 512], BF16)
    kT = [sb("kT0", [128, 640], BF16), sb("kT1", [128, 640], BF16)]
    V = sb("V", [128, 5, 2, 66], BF16)
    uhalo = sb("uhalo", [128, 8, 30], BF16)
    rawhalo = sb("rawhalo", [128, 44, 2], BF16)
    prm = sb("prm_sb", [128, NPRM], F32)
    rowp = sb("rowp_sb", [128, NROW], F32)
    esink = sb("esink", [128, 16], F32)
    ident = sb("ident", [128, 128], BF16)
    ones = sb("ones", [128, 128], BF16)
    maskP = sb("maskP", [128, 4, 128], BF16)
    maskC = sb("maskC", [128, 4, 128], BF16)
    junk = sb("junk", [128, 1024], BF16)
    st = sb("st", [128, 64], F32)
    mhalf = sb("mhalf", [128, 8], F32)
    hb = sb("hb", [128, 16], F32)
    arena = sb("arena", [128, 51440], BF16)
    ps = nc.alloc_psum_tensor("ps", [128, 8, 512], F32).ap()

    def sub(off, n):
        return arena[:, off:off + n]

    uT = sub(0, 4336).rearrange("p (c t) -> p c t", c=8)
    ucT = sub(4336, 4096).rearrange("p (c t) -> p c t", c=8)
    usq = sub(8432, 1024).rearrange("p (c t) -> p c t", c=2)
    qT = sub(9456, 4096).rearrange("p (c t) -> p c t", c=8)
    gcT = sub(13552, 4096).rearrange("p (c t) -> p c t", c=8)
    gaT = sub(17648, 4096).rearrange("p (c t) -> p c t", c=8)
    PT = sub(21744, 4096).rearrange("p (s k t) -> p s k t", s=2, k=2)
    attn_tm = sub(25840, 2048).rearrange("p (s t) -> p s t", s=2)
    attnT = sub(27888, 4096).rearrange("p (c t) -> p c t", c=8)
    mergedT = sub(31984, 4096).rearrange("p (c t) -> p c t", c=8)
    diag = sub(36080, 4096).rearrange("p (s k t) -> p s k t", s=2, k=16)
    Rm = sub(36080, 128)
    posi = sub(40176, 1024).bitcast(I32)
    cosT = sub(41200, 1024).bitcast(F32)
    sinT = sub(42224, 1024).bitcast(F32)
    rt = [sub(43248, 1024).bitcast(F32), sub(44272, 1024).bitcast(F32)]
    tmpA = sub(45296, 2048).bitcast(F32).rearrange("p (s t) -> p s t", s=2)
    tmpB = sub(47344, 2048).bitcast(F32).rearrange("p (s t) -> p s t", s=2)
    sig = sub(49392, 2048).bitcast(F32).rearrange("p (s t) -> p s t", s=2)
    wEs = sub(0, 22528).rearrange("p (h k t) -> p h k t", h=2, k=22)
    zT = sub(22528, 11264).rearrange("p (c t) -> p c t", c=22)
    raw = sub(33792, 2064).rearrange("p (s t) -> p s t", s=4)
    sg = sub(35856, 2048).bitcast(F32).rearrange("p (s t) -> p s t", s=2)
    fdring = sub(39440, 7680).rearrange("p (s m t) -> p s m t", s=2, m=30)
    stin = sub(0, 8192).bitcast(F32).rearrange("p (s t) -> p s t", s=2)
    stout = sub(8192, 4096).rearrange("p (s t) -> p s t", s=2)
    identf = sub(12288, 1024).bitcast(F32)

    P = Prog(nc)
    R = P.region
    R("uTh", 0, 4336)
    for c in range(8):
        R("uT%d" % c, c * 542, (c + 1) * 542)
        R("uc%d" % c, 4336 + c * 512, 4336 + (c + 1) * 512)
        R("qT%d" % c, 9456 + c * 512, 9456 + (c + 1) * 512)
        R("g%d" % c, 13552 + c * 512, 13552 + (c + 1) * 512)
        R("g%d" % (8 + c), 17648 + c * 512, 17648 + (c + 1) * 512)
        R("mg%d" % c, 31984 + c * 512, 31984 + (c + 1) * 512)
    for sl in range(2):
        R("usq%d" % sl, 8432 + sl * 512, 8432 + (sl + 1) * 512)
        for kb in range(2):
            for half in range(2):
                o = 21744 + sl * 2048 + kb * 1024 + half * 512
                R("PT%d_%d_%d" % (sl, kb, half), o, o + 512)
        for g in range(2):
            for b2 in range(2):
                o = 25840 + sl * 1024 + (g * 8 + 4 * b2) * 64
                R("atm%d_%d_%d" % (sl, g, b2), o, o + 256)
        R("tA%d" % sl, 45296 + sl * 1024, 45296 + (sl + 1) * 1024)
        R("tB%d" % sl, 47344 + sl * 1024, 47344 + (sl + 1) * 1024)
        R("sig%d" % sl, 49392 + sl * 1024, 49392 + (sl + 1) * 1024)
        R("rt%d" % sl, 43248 + sl * 1024, 43248 + (sl + 1) * 1024)
        R("wE%d" % sl, sl * 11264, (sl + 1) * 11264)
        R("sg%d" % sl, 35856 + sl * 1024, 35856 + (sl + 1) * 1024)
        R("fdr%d" % sl, 39440 + sl * 3840, 39440 + (sl + 1) * 3840)
        R("stin%d" % sl, sl * 4096, (sl + 1) * 4096)
        R("stout%d" % sl, 8192 + sl * 2048, 8192 + (sl + 1) * 2048)
    for n in range(4):
        R("aT%d" % n, 27888, 31984)
        R("rawh%d" % n, 33792 + n * 516, 33792 + n * 516 + 2)
        R("raw%d" % n, 33792 + n * 516 + 2, 33792 + (n + 1) * 516)
    R("fx0", 39440, 40464)
    R("fx1", 40464, 41488)
    R("Rm", 36080, 36208)
    R("posi", 40176, 41200)
    R("cosT", 41200, 42224)
    R("sinT", 42224, 43248)
    for j in range(22):
        R("z%d" % j, 22528 + j * 512, 22528 + (j + 1) * 512)
    R("identf", 12288, 13312)

    def DMA(q, out, in_, reads=(), writes=(), extra=()):
        return P.op(q, lambda e: e.dma_start(out=out, in_=in_), reads, writes, dma=True, extra=extra)

    def ACT(out, in_, func, reads, writes, scale=1.0, bias=0.0, accum=None):
        def fn(e):
            if accum is not None:
                return e.activation(out=out, in_=in_, func=func, scale=scale, bias=bias, accum_out=accum)
            return e.activation(out=out, in_=in_, func=func, scale=scale, bias=bias)
        return P.op("act", fn, reads, writes)

    def TS(eng, out, in0, s1, s2, op0, op1, reads, writes):
        def fn(e):
            if s2 is None:
                return e.tensor_scalar(out=out, in0=in0, scalar1=s1, scalar2=None, op0=op0)
            return e.tensor_scalar(out=out, in0=in0, scalar1=s1, scalar2=s2, op0=op0, op1=op1)
        return P.op(eng, fn, reads, writes)

    def STT(out, in0, scalar, in1, op0, op1, reads, writes):
        return P.op("dve", lambda e: e.scalar_tensor_tensor(out=out, in0=in0, scalar=scalar, in1=in1, op0=op0, op1=op1),
                    reads, writes)

    def TT(eng, out, in0, in1, op, reads, writes):
        return P.op(eng, lambda e: e.tensor_tensor(out=out, in0=in0, in1=in1, op=op), reads, writes)

    def CP(eng, out, in_, reads, writes):
        return P.op(eng, lambda e: e.tensor_copy(out=out, in_=in_), reads, writes)

    def MEMSET(eng, ap, val, writes):
        return P.op(eng, lambda e: e.memset(ap, val), (), writes)

    def MM(out, pairs, reads, writes, start=True, stop=True):
        def fn(e):
            n = len(pairs)
            ins = None
            for i, (l, r) in enumerate(pairs):
                ins = e.matmul(out, lhsT=l, rhs=r, start=(start and i == 0), stop=(stop and i == n - 1))
            return ins
        return P.op("pe", fn, reads, writes)

    def TRN(outs_ins, reads, writes):
        def fn(e):
            ins = None
            for (o, i_) in outs_ins:
                ins = e.transpose(out=o, in_=i_, identity=ident)
            return ins
        return P.op("pe", fn, reads, writes)

    bank_ctr = [0]

    def next_bank():
        b = bank_ctr[0] % 4
        bank_ctr[0] += 1
        return b

    wctr = [0]

    def wload(src):
        slot = wctr[0] % NWSLOT
        wctr[0] += 1
        w = src.shape[-1]
        DMA("sp", wring[:, slot, :, 0:w], src, (), ["wr%d" % slot])
        return slot

    def wload_flat(src):
        slot = wctr[0] % NWSLOT
        wctr[0] += 1
        DMA("sp", wflat[:, slot, 0:src.shape[-1]], src, (), ["wr%d" % slot])
        return slot

    def dgview(slot):
        return wflat[:, slot, :].rearrange("p (m t) -> p m t", m=32)

    def pcol(off, j=0):
        return prm[:, off + j:off + j + 1]

    DMA("sp", prm, prm_d, (), ["prm"])
    DMA("sp", rowp, rowp_d.partition_broadcast(128), (), ["rowp"])
    P.op("pool", lambda e: e.iota(identf, pattern=[[0, 4], [1, 128]], base=0, channel_multiplier=-1,
                                  allow_small_or_imprecise_dtypes=True), (), ["identf"])
    TS("dve", ident, identf[:, 0:128], 0.0, None, ALU.is_equal, None, ["identf"], ["ident"])
    NEG = -30000.0
    TS("dve", maskC.rearrange("p a b -> p (a b)"), identf, 0.0, NEG, ALU.is_lt, ALU.mult, ["identf"], ["maskC"])
    TS("dve", maskP.rearrange("p a b -> p (a b)"), identf, 0.0, NEG, ALU.is_ge, ALU.mult, ["identf"], ["maskP"])
    MEMSET("dve", ones, 1.0, ["ones"])
    MEMSET("dve", mhalf, -0.5, ["mhalf"])
    MEMSET("dve", kT[0], 0.0, ["kT"])
    MEMSET("dve", kT[1], 0.0, ["kT"])
    MEMSET("dve", V.rearrange("p a b c -> p (a b c)"), 1.0, ["V"])
    ACT(esink, rowp[:, 2048:2064], AF.Exp, ["rowp"], ["esink"])
    TS("dve", hb, prm[:, O_BG:O_BG + 16], 0.5, None, ALU.mult, None, ["prm"], ["hb"])
    TS("dve", rowp[:, 0:1024], rowp[:, 0:1024], 0.5, None, ALU.mult, None, ["rowp"], ["rowp"])

    stores = []
    pc = [0]

    def prep(src, dst, KC, C, gain_off):
        for kc in range(KC):
            c0 = 0
            while c0 < C:
                cw = min(2048, C - c0)
                if cw > 512 and cw % 512:
                    cw = (cw // 512) * 512
                slot = pc[0] % 2
                DMA("sp", stin[:, slot, 0:cw], src[kc * 128:(kc + 1) * 128, c0:c0 + cw], (), ["stin%d" % slot])
                eng = "act" if pc[0] % 2 == 0 else "dve"
                o_ap = stout[:, slot, 0:cw]
                i_ap = stin[:, slot, 0:cw]
                rd = ["stin%d" % slot, "prm"]
                wr = ["stout%d" % slot]
                if gain_off is None:
                    if eng == "act":
                        ACT(o_ap, i_ap, AF.Copy, rd, wr)
                    else:
                        CP("dve", o_ap, i_ap, rd, wr)
                else:
                    g = pcol(gain_off, kc)
                    if eng == "act":
                        ACT(o_ap, i_ap, AF.Identity, rd, wr, scale=g)
                    else:
                        TS("dve", o_ap, i_ap, g, None, ALU.mult, None, rd, wr)
                u0 = c0 // 512
                w = min(512, cw)
                nu = cw // w
                d_ap = dst[u0:u0 + nu, :, kc, 0:w].rearrange("u p c -> p u c")
                s_ap = stout[:, slot, 0:cw].rearrange("p (u c) -> p u c", u=nu)
                stores.append(DMA("act", d_ap, s_ap, ["stout%d" % slot], ()))
                pc[0] += 1
                c0 += cw

    stbf = sub(0, 8192).rearrange("p (s m t) -> p s m t", s=2, m=32)

    def build_diags(dst, cols):
        slot = pc[0] % 2
        pc[0] += 1

        def fn(e):
            ins = None
            for m, col in enumerate(cols):
                ins = e.tensor_scalar(out=stbf[:, slot, m, :], in0=ident, scalar1=prm[:, col:col + 1],
                                      scalar2=None, op0=ALU.mult)
            return ins
        P.op("dve", fn, ["ident", "prm"], ["stin%d" % slot])
        n = len(cols)
        stores.append(DMA("act", dst[:, 0:n * 128], stbf[:, slot, 0:n, :].rearrange("p m t -> p (m t)"),
                          ["stin%d" % slot], ()))

    prep(win_d, wA, 8, 5376, O_G1)
    for c in range(8):
        build_diags(wF[c], [O_CW + c * 31 + k for k in range(31)])
    prep(wb_d, wB, 8, 2048, None)
    prep(wout_d, wC, 8, 1024, None)
    for u in range(5):
        chs = range(10 * u, min(44, 10 * u + 10))
        build_diags(wG[u], [O_FW + ch * 3 + k for ch in chs for k in range(3)])
    prep(wup_d, wD, 8, 5632, O_G3)
    prep(wdn_d, wE, 22, 1024, None)

    def load_x(i):
        b = i % 2
        t0 = i * 512
        DMA("sp", xbuf[:, b, :, :], x_d[t0:t0 + 512, :].rearrange("(s p) d -> p s d", p=128),
            (), ["xb%ds%d" % (b, s) for s in range(4)])

    def pre_stats(b, s):
        ACT(junk, xbuf[:, b, s, :], AF.Square, ["xb%ds%d" % (b, s)], ["junk", "ssq%d" % s], accum=st[:, s:s + 1])
        TS("dve", st[:, 4 + s:5 + s], st[:, s:s + 1], 1.0 / 1024, 1e-6, ALU.mult, ALU.add, ["ssq%d" % s], ["ms%d" % s])
        TT("pool", st[:, 8 + s:9 + s], st[:, 4 + s:5 + s], mhalf[:, 0:1], ALU.pow, ["ms%d" % s, "mhalf"], ["rstd%d" % s])

    def pre_h(b, s):
        TS("dve", hTm[:, s, :], xbuf[:, b, s, :], st[:, 8 + s:9 + s], None, ALU.mult, None,
           ["xb%ds%d" % (b, s), "rstd%d" % s], ["hTm%d" % s])

    def pre_TR(s, which):
        hT = hTa if which == "a" else hTb
        tb = 4 if s % 2 == 0 else 7
        pst = ps[:, tb, :].bitcast(BF16)
        TRN([(pst[:, kc * 128:(kc + 1) * 128], hTm[:, s, kc * 128:(kc + 1) * 128]) for kc in range(8)],
            ["hTm%d" % s, "ident"], ["ps%d" % tb])
        if s % 2:
            CP("dve", hT[:, :, s * 128:(s + 1) * 128], pst.rearrange("p (k t) -> p k t", k=8), ["ps%d" % tb],
               ["hT%s%d" % (which, s)])
        else:
            ACT(hT[:, :, s * 128:(s + 1) * 128], pst.rearrange("p (k t) -> p k t", k=8), AF.Copy, ["ps%d" % tb],
                ["hT%s%d" % (which, s)])

    def pre_T(b, s, which):
        pre_h(b, s)
        pre_TR(s, which)

    HTA = ["hTa%d" % s for s in range(4)]
    HTB = ["hTb%d" % s for s in range(4)]

    def rope_tables(i):
        seq, ti = i // 4, i % 4
        DMA("sp", posi, pos_d[seq:seq + 1, ti * 512:(ti + 1) * 512].partition_broadcast(128), (), ["posi"])
        ang = tmpA[:, 0, :]
        a2 = tmpA[:, 1, :]
        kf = tmpB[:, 0, :]
        ki = tmpB[:, 1, :].bitcast(I32)
        C1 = 6.28125
        C2 = float(2 * np.pi - 6.28125)
        CL = 3.1415925
        CP("dve", ang, posi, ["posi"], ["tA0"])
        TS("dve", ang, ang, pcol(O_IF), None, ALU.mult, None, ["tA0", "prm"], ["tA0"])
        for (tab, shift, key) in ((sinT, 0.0, "sinT"), (cosT, PI / 2, "cosT")):
            TS("dve", a2, ang, shift, None, ALU.add, None, ["tA0"], ["tA1"])
            TS("dve", kf, a2, float(1.0 / (2 * np.pi)), None, ALU.mult, None, ["tA1"], ["tB0"])
            CP("dve", ki, kf, ["tB0"], ["tB1"])
            CP("dve", kf, ki, ["tB1"], ["tB0"])
            STT(a2, kf, -C1, a2, ALU.mult, ALU.add, ["tB0", "tA1"], ["tA1"])
            STT(a2, kf, -C2, a2, ALU.mult, ALU.add, ["tB0", "tA1"], ["tA1"])
            TS("dve", a2, a2, CL, -CL, ALU.min, ALU.max, ["tA1"], ["tA1"])
            if key == "sinT":
                ACT(tab, a2, AF.Sin, ["tA1", "prm"], [key], scale=pcol(O_SG))
            else:
                ACT(tab, a2, AF.Sin, ["tA1"], [key])

    def proj_chunk(slot, j, which="a"):
        b = next_bank()
        hT = hTa if which == "a" else hTb
        MM(ps[:, b, :], [(wring[:, slot, kc, j * 128:(j + 1) * 128], hT[:, kc, :]) for kc in range(8)],
           ["wr%d" % slot] + (HTA if which == "a" else HTB), ["ps%d" % b])
        return b

    def phase_A1(i):
        first = (i % 4 == 0)
        if not first:
            for g in range(2):
                CP("pool", kT[g][:, 0:128], kT[g][:, 512:640], ["kT"], ["kT"])
            CP("pool", V[:, 0, :, 0:64], V[:, 4, :, 0:64], ["V"], ["V"])
            CP("pool", uT[:, :, 0:30], uhalo, ["uhalo"], ["uTh"])
        else:
            MEMSET("pool", uT[:, :, 0:30], 0.0, ["uTh"])
        for u in range(4):
            slot = wload(wA[u])
            for half in range(2):
                c = 2 * u + half
                bv = proj_chunk(slot, 2 * half)
                bg = proj_chunk(slot, 2 * half + 1)
                sl = c % 2
                ACT(sig[:, sl, :], ps[:, bg, :], AF.Tanh, ["ps%d" % bg], ["sig%d" % sl], scale=0.5)
                STT(uT[:, c, 30:542], sig[:, sl, :], 1.0, ps[:, bv, :], ALU.add, ALU.mult,
                    ["ps%d" % bv, "sig%d" % sl], ["uT%d" % c])

    def phase_A2(i):
        for (d0, s0) in ((0, 32), (32, 0), (64, 96), (96, 64)):
            CP("dve", Rm[:, d0:d0 + 32], ident[:, s0:s0 + 32], ["ident"], ["Rm"])

        def rope_a(bq, sl):
            ACT(usq[:, sl, :], ps[:, bq, :], AF.Copy, ["ps%d" % bq], ["usq%d" % sl])

        def rope_b(bq, sl, r0, r1, k0, k1, fin):
            br = next_bank()
            MM(ps[:, br, :], [(Rm, usq[:, sl, :])], ["Rm", "usq%d" % sl], ["ps%d" % br])
            TT("dve", r0, ps[:, bq, :], cosT, ALU.mult, ["ps%d" % bq, "cosT"], [k0])
            TT("dve", r1, ps[:, br, :], sinT, ALU.mult, ["ps%d" % br, "sinT"], [k1])
            fin()

        pend = [None]

        def flush():
            if pend[0] is not None:
                rope_b(*pend[0])
                pend[0] = None

        for u in range(2):
            slot = wload(wA[4 + u])
            for j in range(4):
                c = 4 * u + j
                bq = proj_chunk(slot, j)
                rope_a(bq, c % 2)
                flush()
                if c % 2 == 0:
                    r0, r1, k0, k1 = rt[0], rt[1], "rt0", "rt1"
                else:
                    r0, r1, k0, k1 = sig[:, 0, :], sig[:, 1, :], "sig0", "sig1"

                def fin(c=c, r0=r0, r1=r1, k0=k0, k1=k1):
                    TT("pool", qT[:, c, :], r0, r1, ALU.add, [k0, k1], ["qT%d" % c])
                    ln_apply(c)
                pend[0] = (bq, c % 2, r0, r1, k0, k1, fin)
        slot = wload(wA[10][:, :, 0:256])
        bq = proj_chunk(slot, 0)
        rope_a(bq, 0)
        flush()

        def fink():
            TT("pool", kT[0][0:64, 128:640], rt[0][0:64, :], rt[1][0:64, :], ALU.add, ["rt0", "rt1"], ["kT"])
            TT("pool", kT[1][64:128, 128:640], rt[0][64:128, :], rt[1][64:128, :], ALU.add, ["rt0", "rt1"], ["kT"])
        pend[0] = (bq, 0, rt[0], rt[1], "rt0", "rt1", fink)
        for s in range(4):
            MM(ps[:, 7, s * 128:(s + 1) * 128],
               [(hTa[:, kc, s * 128:(s + 1) * 128], wring[:, slot, kc, 128:256]) for kc in range(8)],
               ["wr%d" % slot] + HTA, ["ps7"])
        ACT(V[:, 1:5, :, 0:64], ps[:, 7, :].rearrange("p (s g d) -> p s g d", s=4, g=2), AF.Copy, ["ps7"], ["V"])
        flush()
        for u in range(4):
            slot = wload(wA[6 + u])
            for j in range(4):
                cc = 4 * u + j
                b = proj_chunk(slot, j)
                dst = gcT[:, cc, :] if cc < 8 else gaT[:, cc - 8, :]
                ACT(dst, ps[:, b, :], AF.Tanh, ["ps%d" % b, "hb"], ["g%d" % cc], scale=0.5, bias=hb[:, cc:cc + 1])

    dctr = [0]

    def phase_B(i):
        CP("pool", uhalo, uT[:, :, 512:542], ["uT%d" % c for c in range(8)], ["uhalo"])

        def stats(c):
            sl = c % 2
            MM(ps[:, 5, :], [(ones, ucT[:, c, :])], ["ones", "uc%d" % c], ["ps5"], start=(c == 0), stop=(c == 7))
            MM(ps[:, 6, :], [(ones, usq[:, sl, :])], ["ones", "usq%d" % sl], ["ps6"], start=(c == 0), stop=(c == 7))

        for c in range(8):
            b = next_bank()
            dslot = wload_flat(wF[c][:, 0:31 * 128])
            dv = dgview(dslot)
            MM(ps[:, b, :], [(dv[:, k, :], uT[:, c, k:k + 512]) for k in range(31)],
               ["wr%d" % dslot, "uT%d" % c, "uTh"], ["ps%d" % b])
            sl = c % 2
            ACT(ucT[:, c, :], ps[:, b, :], AF.Identity, ["ps%d" % b, "prm"], ["uc%d" % c], scale=0.5, bias=pcol(O_CB, c))
            ACT(usq[:, sl, :], ps[:, b, :], AF.Square, ["ps%d" % b, "prm"], ["usq%d" % sl], scale=0.5, bias=pcol(O_CB, c))
            if c >= 1:
                stats(c - 1)
        stats(7)

    def ln_head():
        mean = tmpA[:, 0, :]
        m2 = tmpA[:, 1, :]
        var = tmpB[:, 0, :]
        TS("dve", mean, ps[:, 5, :], 1.0 / 1024, None, ALU.mult, None, ["ps5"], ["tA0"])
        ACT(m2, mean, AF.Square, ["tA0"], ["tA1"])
        STT(var, ps[:, 6, :], 1.0 / 1024, m2, ALU.mult, ALU.subtract, ["ps6", "tA1"], ["tB0"])
        TS("dve", var, var, 1e-5, None, ALU.add, None, ["tB0"], ["tB0"])
        ACT(var, var, AF.Sqrt, ["tB0"], ["tB0"])
        P.op("dve", lambda e: e.reciprocal(out=ps[:, 5, :], in_=var), ["tB0"], ["ps5"])
        STT(ps[:, 6, :], mean, -1.0, ps[:, 5, :], ALU.mult, ALU.mult, ["tA0", "ps5"], ["ps6"])

    def ln_apply(c):
        sl = c % 2
        tk = "tB1" if sl else "tA1"
        tb = tmpB[:, 1, :] if sl else tmpA[:, 1, :]
        TT("dve", tb, ucT[:, c, :], ps[:, 5, :], ALU.mult, ["uc%d" % c, "ps5"], [tk])
        TT("dve", tb, tb, ps[:, 6, :], ALU.add, [tk, "ps6"], [tk])
        ACT(ucT[:, c, :], tb, AF.Silu, [tk, "prm"], ["uc%d" % c], scale=pcol(O_LG, c), bias=pcol(O_LB, c))

    sbank = [0]

    def phase_C(i):
        first = (i % 4 == 0)
        OB = (3, 7)

        def unit_S(k):
            n, g = k // 2, k % 2
            has_prev = not (first and n == 0)
            kbs = ([0] if has_prev else []) + [1]
            psl = k % 2
            for kb in kbs:
                kcols = slice(128 * (n + kb), 128 * (n + kb) + 128)
                msk = maskP if kb == 0 else maskC
                for half in range(2):
                    b = sbank[0] % 3
                    sbank[0] += 1
                    MM(ps[:, b, :], [(kT[g][:, kcols], qT[:, 4 * half:4 * half + 4, 128 * n:128 * n + 128]),
                                     (ident, msk)],
                       ["kT", "ident", "maskP", "maskC"] + ["qT%d" % c for c in range(4 * half, 4 * half + 4)],
                       ["ps%d" % b])
                    ACT(PT[:, psl, kb, half * 512:(half + 1) * 512], ps[:, b, :], AF.Exp, ["ps%d" % b],
                        ["PT%d_%d_%d" % (psl, kb, half)], scale=0.125)

        def unit_PV(k):
            n, g = k // 2, k % 2
            has_prev = not (first and n == 0)
            kbs = ([0] if has_prev else []) + [1]
            psl = k % 2
            asl = n % 2
            for bank2 in range(2):
                ob = OB[bank2]
                for cc in range(4):
                    c = 4 * bank2 + cc
                    MM(ps[:, ob, cc * 65:cc * 65 + 65],
                       [(PT[:, psl, kb, c * 128:(c + 1) * 128], V[:, n + kb, g, 0:65]) for kb in kbs],
                       ["V"] + ["PT%d_%d_%d" % (psl, kb, bank2) for kb in kbs], ["ps%d" % ob])
                h0 = g * 8 + 4 * bank2
                o3 = ps[:, ob, 0:260].rearrange("p (c d) -> p c d", c=4)
                den = st[:, 16 + 4 * bank2:20 + 4 * bank2]
                TT("dve", den, o3[:, :, 64], esink[:, h0:h0 + 4], ALU.add, ["ps%d" % ob, "esink"], ["den%d" % bank2])
                P.op("dve", lambda e, den=den: e.reciprocal(out=den, in_=den), ["den%d" % bank2], ["den%d" % bank2])

                def nfn(e, bank2=bank2, h0=h0, o3=o3, asl=asl):
                    ins = None
                    for cc in range(4):
                        h = h0 + cc
                        ins = e.tensor_scalar(out=attn_tm[:, asl, h * 64:(h + 1) * 64], in0=o3[:, cc, 0:64],
                                              scalar1=st[:, 16 + 4 * bank2 + cc:17 + 4 * bank2 + cc],
                                              scalar2=None, op0=ALU.mult)
                    return ins
                P.op("dve", nfn, ["ps%d" % ob, "den%d" % bank2], ["atm%d_%d_%d" % (asl, g, bank2)])

        def unit_T(n):
            asl = n % 2
            pst = ps[:, 4, :].bitcast(BF16)
            TRN([(pst[:, kc * 128:(kc + 1) * 128], attn_tm[:, asl, kc * 128:(kc + 1) * 128]) for kc in range(8)],
                ["atm%d_%d_%d" % (asl, g, b2) for g in range(2) for b2 in range(2)] + ["ident"], ["ps4"])
            ACT(attnT[:, :, n * 128:(n + 1) * 128], pst.rearrange("p (k t) -> p k t", k=8), AF.Copy, ["ps4"], ["aT%d" % n])

        unit_S(0)
        for k in range(8):
            if k + 1 < 8:
                unit_S(k + 1)
            unit_PV(k)
            if k >= 2 and k % 2 == 0:
                unit_T(k // 2 - 1)
        unit_T(3)

    def post_norm(i, wslots_or_views, nk, lhs_buf, lhs_keys, goff, tms, hook_a=None, hook_pe=None):
        b = i % 2

        def stage1(s):
            mine = tms[2 * (s % 2):2 * (s % 2) + 2]
            for half in range(2):
                bk = next_bank()
                tm, tk = mine[half]
                MM(ps[:, bk, :], [(lhs_buf[:, kc, s * 128:(s + 1) * 128], wslots_or_views[half][0][:, kc, :]) for kc in range(nk)],
                   lhs_keys + [wslots_or_views[half][1]], ["ps%d" % bk])
                ACT(junk[:, 0:512], ps[:, bk, :], AF.Square, ["ps%d" % bk], ["junk", "pq%d_%d" % (s, half)],
                    accum=st[:, 32 + 2 * s + half:33 + 2 * s + half])
                TT("dve", tm, ps[:, bk, :], rowp[:, goff + half * 512:goff + (half + 1) * 512], ALU.mult,
                   ["ps%d" % bk, "rowp"], [tk])

        def stage2(s):
            mine = tms[2 * (s % 2):2 * (s % 2) + 2]
            TT("dve", st[:, 40 + s:41 + s], st[:, 32 + 2 * s:33 + 2 * s], st[:, 33 + 2 * s:34 + 2 * s], ALU.add,
               ["pq%d_0" % s, "pq%d_1" % s], ["pms%d" % s])
            TS("dve", st[:, 40 + s:41 + s], st[:, 40 + s:41 + s], (1.0 / 4096) if goff == 0 else (1.0 / 1024), 1e-6,
               ALU.mult, ALU.add, ["pms%d" % s], ["pms%d" % s])
            TT("pool", st[:, 28 + s:29 + s], st[:, 40 + s:41 + s], mhalf[:, 0:1], ALU.pow, ["pms%d" % s, "mhalf"], ["prs%d" % s])
            for half in range(2):
                tm, tk = mine[half]
                xk = "xb%ds%d" % (b, s)
                STT(xbuf[:, b, s, half * 512:(half + 1) * 512], tm, st[:, 28 + s:29 + s],
                    xbuf[:, b, s, half * 512:(half + 1) * 512], ALU.mult, ALU.add, [tk, xk, "prs%d" % s], [xk])
            if hook_a is not None:
                hook_a(s)

        for s in range(4):
            stage1(s)
            if s >= 1:
                stage2(s - 1)
            if hook_pe is not None and s >= 2:
                hook_pe[0](s - 2)
        stage2(3)
        if hook_pe is not None:
            hook_pe[0](2)
            hook_pe[0](3)
            for s in range(4):
                hook_pe[1](s)

    def phase_D(i):
        for u in range(4):
            slot = wload(wB[u])
            for half in range(2):
                c = 2 * u + half
                ba = next_bank()
                MM(ps[:, ba, :], [(wring[:, slot, kc, (2 * half) * 128:(2 * half + 1) * 128], ucT[:, kc, :]) for kc in range(8)],
                   ["wr%d" % slot] + ["uc%d" % k for k in range(8)], ["ps%d" % ba])
                bb = next_bank()
                MM(ps[:, bb, :], [(wring[:, slot, kc, (2 * half + 1) * 128:(2 * half + 2) * 128], attnT[:, kc, :]) for kc in range(8)],
                   ["wr%d" % slot] + ["aT%d" % n for n in range(4)], ["ps%d" % bb])
                sl = c % 2
                STT(tmpA[:, sl, :], gcT[:, c, :], 1.0, ps[:, ba, :], ALU.add, ALU.mult, ["ps%d" % ba, "g%d" % c], ["tA%d" % sl])
                STT(tmpB[:, sl, :], gaT[:, c, :], 1.0, ps[:, bb, :], ALU.add, ALU.mult, ["ps%d" % bb, "g%d" % (8 + c)], ["tB%d" % sl])
                TT("pool", mergedT[:, c, :], tmpA[:, sl, :], tmpB[:, sl, :], ALU.add, ["tA%d" % sl, "tB%d" % sl], ["mg%d" % c])
        s0 = wload(wC[0])
        s1 = wload(wC[1])
        b = i % 2
        post_norm(i, [(wring[:, s0, :, :], "wr%d" % s0), (wring[:, s1, :, :], "wr%d" % s1)], 8, mergedT,
                  ["mg%d" % c for c in range(8)], 0,
                  [(tmpA[:, 0, :], "tA0"), (tmpA[:, 1, :], "tA1"), (tmpB[:, 0, :], "tB0"), (tmpB[:, 1, :], "tB1")],
                  hook_a=lambda s: pre_stats(b, s), hook_pe=(lambda s: pre_h(b, s), lambda s: pre_TR(s, "b")))

    rctr = [0]

    def phase_E(i):
        first = (i % 4 == 0)
        b = i % 2
        if i + 1 < NT:
            load_x(i + 1)
        pend = [None]

        def conv_part(rs, ch, which, j2, sgl, fslot):
            bc = next_bank()
            MM(ps[:, bc, :], [(fdring[:, fslot, (ch % 10) * 3 + k, :], raw[:, rs, k:k + 512]) for k in range(3)],
               ["raw%d" % rs, "rawh%d" % rs, "fdr%d" % fslot], ["ps%d" % bc])
            if which == 0:
                ACT(sg[:, sgl, :], ps[:, bc, :], AF.Silu, ["ps%d" % bc, "prm"], ["sg%d" % sgl], bias=pcol(O_FB, ch))
            else:
                STT(zT[:, j2, :], ps[:, bc, :], pcol(O_FB, ch), sg[:, sgl, :], ALU.add, ALU.mult,
                    ["ps%d" % bc, "sg%d" % sgl, "prm"], ["z%d" % j2])

        def fload(uu):
            nch = min(44, 10 * uu + 10) - 10 * uu
            fs = uu % 2
            DMA("sp", fdring[:, fs, 0:nch * 3, :], wG[uu][:, 0:nch * 3 * 128].rearrange("p (m t) -> p m t", t=128),
                (), ["fdr%d" % fs])
            return fs

        fsl = [fload(0)]
        fnext = [None]
        for u in range(11):
            slot = wload(wD[u])
            if u == 2:
                for hh in range(2):
                    DMA("sp", wEs[:, hh, :, :], wE[hh], (), ["wE%d" % hh])
            if u == 4 and i + 1 < NT:
                for s in range(4):
                    pre_stats((i + 1) % 2, s)
            if u in (5, 6, 7, 8) and i + 1 < NT:
                pre_T((i + 1) % 2, u - 5, "a")
            for half in range(2):
                j2 = 2 * u + half
                sgl = j2 % 2
                for which in range(2):
                    ch = 2 * j2 + which
                    bk = proj_chunk(slot, 2 * half + which, "b")
                    rs = rctr[0] % 4
                    rctr[0] += 1
                    ACT(raw[:, rs, 2:514], ps[:, bk, :], AF.Copy, ["ps%d" % bk], ["raw%d" % rs])

                    if ch % 10 == 0 and ch > 0:
                        fsl[0] = fnext[0]
                    if (ch + 4) % 10 == 0 and ch + 4 < 44:
                        fnext[0] = fload((ch + 4) // 10)
                    if first:
                        MEMSET("pool", raw[:, rs, 0:2], 0.0, ["rawh%d" % rs])
                    else:
                        CP("pool", raw[:, rs, 0:2], rawhalo[:, ch, :], ["rh%d" % ch], ["rawh%d" % rs])
                    CP("pool", rawhalo[:, ch, :], raw[:, rs, 512:514], ["raw%d" % rs], ["rh%d" % ch])
                    if pend[0] is not None:
                        conv_part(*pend[0])
                    pend[0] = (rs, ch, which, j2, sgl, fsl[0])
        conv_part(*pend[0])
        t0 = i * 512
        outs = []

        def store(s):
            outs.append(DMA("act", y_d[t0 + s * 128:t0 + (s + 1) * 128, :], xbuf[:, b, s, :], ["xb%ds%d" % (b, s)], ()))
        post_norm(i, [(wEs[:, 0, :, :], "wE0"), (wEs[:, 1, :, :], "wE1")], 22, zT,
                  ["z%d" % j for j in range(22)], 1024,
                  [(sg[:, 0, :], "sg0"), (sg[:, 1, :], "sg1"),
                   (sub(39440, 1024).bitcast(F32), "fx0"), (sub(40464, 1024).bitcast(F32), "fx1")],
                  hook_a=store)
        return outs

    load_x(0)
    P.fence(extra=stores)
    final = []
    for s in range(4):
        pre_stats(0, s)
    for s in range(4):
        pre_T(0, s, "a")
    for i in range(NT):
        phase_A1(i)
        rope_tables(i)
        phase_B(i)
        ln_head()
        phase_A2(i)
        phase_C(i)
        phase_D(i)
        final += phase_E(i)
    P.emit(final_waits=final)
    return nc


def _host_layout(inputs):
    f = np.float32
    w_in = np.asarray(inputs["w_in"], f)[0]
    cols = []
    for c in range(8):
        cols += list(range(c * 128, (c + 1) * 128)) + list(range(1024 + c * 128, 1024 + (c + 1) * 128))
    d = np.arange(64)
    for c in range(8):
        for h in (c, 8 + c):
            cols += list(2048 + h * 64 + d)
    cols += list(range(3328, 5376))
    cols += list(range(3072, 3200))
    cols += list(range(3200, 3328))
    w_in_p = np.ascontiguousarray(w_in[:, np.array(cols)])
    wco = np.asarray(inputs["w_conv_out"], f)[0]
    wao = np.asarray(inputs["w_attn_out"], f)[0]
    w_b_p = np.ascontiguousarray(np.concatenate(
        [m[:, c * 128:(c + 1) * 128] for c in range(8) for m in (wco, wao)], axis=1))
    w_up = np.asarray(inputs["w_up"], f)[0]
    upcols = []
    for j in range(22):
        upcols += list(range(j * 128, (j + 1) * 128)) + list(range(2816 + j * 128, 2816 + (j + 1) * 128))
    upcols = np.array(upcols)
    w_up_p = np.ascontiguousarray(w_up[:, upcols])
    prm = np.zeros((128, NPRM), f)
    prm[:, O_G1:O_G1 + 8] = np.asarray(inputs["ln_mix_pre"], f)[0].reshape(8, 128).T
    prm[:, O_G3:O_G3 + 8] = np.asarray(inputs["ln_ffn_pre"], f)[0].reshape(8, 128).T
    prm[:, O_BG:O_BG + 16] = np.asarray(inputs["b_gate"], f)[0].reshape(16, 128).T
    cw = np.asarray(inputs["conv_dw_w"], f)[0]
    prm[:, O_CW:O_CW + 248] = cw.T.reshape(8, 128, 31).transpose(1, 0, 2).reshape(128, 248)
    prm[:, O_CB:O_CB + 8] = np.asarray(inputs["conv_dw_b"], f)[0].reshape(8, 128).T
    prm[:, O_LG:O_LG + 8] = np.asarray(inputs["conv_ln_g"], f)[0].reshape(8, 128).T
    prm[:, O_LB:O_LB + 8] = np.asarray(inputs["conv_ln_b"], f)[0].reshape(8, 128).T
    fw = np.asarray(inputs["ffn_dw_w"], f)[0][:, upcols]
    prm[:, O_FW:O_FW + 132] = fw.T.reshape(44, 128, 3).transpose(1, 0, 2).reshape(128, 132)
    prm[:, O_FB:O_FB + 44] = np.asarray(inputs["ffn_dw_b"], f)[0][upcols].reshape(44, 128).T
    p = np.arange(128)
    inv_freq = (10000.0 ** (-(np.arange(0, 64, 2, dtype=np.float32)) / np.float32(64))).astype(f)
    prm[:, O_IF] = inv_freq[p % 32]
    prm[:, O_SG] = np.where((p % 64) < 32, -1.0, 1.0)
    rowp = np.zeros((1, NROW), f)
    rowp[0, 0:1024] = np.asarray(inputs["ln_mix_post"], f)[0]
    rowp[0, 1024:2048] = np.asarray(inputs["ln_ffn_post"], f)[0]
    rowp[0, 2048:2064] = np.asarray(inputs["attn_sinks"], f)[0]
    shared = {
        "w_in_p": w_in_p, "w_b_p": w_b_p, "w_out": np.ascontiguousarray(np.asarray(inputs["w_out"], f)[0]),
        "w_up_p": w_up_p, "w_down": np.ascontiguousarray(np.asarray(inputs["w_down"], f)[0]),
        "prm": prm, "rowp": rowp,
    }
    x = np.asarray(inputs["x"], f)
    pos = np.asarray(inputs["positions"], np.int32)
    in_maps = []
    for c in range(NCORES):
        m = dict(shared)
        m["x"] = np.ascontiguousarray(x[4 * c:4 * c + 4].reshape(8192, 1024))
        m["pos"] = np.ascontiguousarray(pos[4 * c:4 * c + 4])
        in_maps.append(m)
    return in_maps


_NC_CACHE = {}


def kernel(**inputs):
    in_maps = _host_layout(inputs)
    if "nc" not in _NC_CACHE:
        _NC_CACHE["nc"] = build_program()
    nc = _NC_CACHE["nc"]
    res = run_bass_kernel_spmd(nc, in_maps, core_ids=list(range(NCORES)))
    out = np.concatenate([np.asarray(r["y"]).reshape(4, 2048, 1024) for r in res.results], axis=0)
    return out.astype(np.float32)
```

```python
import contextlib
import numpy as np
import concourse.bass as bass
import concourse.mybir as mybir
from concourse.bass_utils import run_bass_kernel_spmd

F32 = mybir.dt.float32
BF16 = mybir.dt.bfloat16
I32 = mybir.dt.int32
AF = mybir.ActivationFunctionType
ALU = mybir.AluOpType

ENGS = ["pe", "act", "dve", "pool", "sp"]
NCORES = 8
NT = 16
NWSLOT = 4
PI = float(np.pi)


class Prog:
    def __init__(self, nc, n_dma_sems=12, strict=("act", "dve", "pool")):
        self.nc = nc
        self.ops = {e: [] for e in ENGS}
        self.lastw = {}
        self.readers = {}
        self.seen = {e: {} for e in ENGS}
        self.seen_dma = {e: set() for e in ENGS}
        self.strict = set(strict)
        self.n_dma_sems = n_dma_sems
        self.ndma = {e: 0 for e in ENGS}
        self.regions = {}
        self.conf = {}

    def region(self, key, lo, hi):
        self.regions[key] = (lo, hi)
        self.conf = {}

    def _conf(self, k):
        c = self.conf.get(k)
        if c is None:
            if k in self.regions:
                lo, hi = self.regions[k]
                c = [k2 for k2, (l2, h2) in self.regions.items() if l2 < hi and lo < h2]
            else:
                c = [k]
            self.conf[k] = c
        return c

    def op(self, eng, fn, reads=(), writes=(), dma=False, extra=()):
        idx = len(self.ops[eng])
        rec = dict(eng=eng, fn=fn, waits={}, dma_waits=[], signal=False, dma=dma, idx=idx)
        deps = set(extra)
        for k0 in reads:
            for k in self._conf(k0):
                w = self.lastw.get(k)
                if w is not None:
                    deps.add(w)
                if k.startswith("ps"):
                    deps.update(r for r in self.readers.get(k, ()) if r[0] != eng)
        for k0 in writes:
            for k in self._conf(k0):
                w = self.lastw.get(k)
                if w is not None:
                    deps.add(w)
                deps.update(self.readers.get(k, ()))
        for (e2, i2) in deps:
            r2 = self.ops[e2][i2]
            if r2["dma"]:
                if (e2, i2) in self.seen_dma[eng]:
                    continue
                self.seen_dma[eng].add((e2, i2))
                rec["dma_waits"].append((e2, i2))
            else:
                if e2 == eng and eng not in self.strict:
                    continue
                if i2 <= self.seen[eng].get(e2, -1):
                    continue
                rec["waits"][e2] = max(rec["waits"].get(e2, -1), i2)
        for e2, i2 in rec["waits"].items():
            self.seen[eng][e2] = i2
            self.ops[e2][i2]["signal"] = True
        if dma:
            rec["dma_j"] = self.ndma[eng]
            self.ndma[eng] += 1
        self.ops[eng].append(rec)
        for k in writes:
            self.lastw[k] = (eng, idx)
            self.readers[k] = []
        for k in reads:
            self.readers.setdefault(k, []).append((eng, idx))
        return (eng, idx)

    def fence(self, extra=()):
        last = []
        for e in ("pe", "act", "dve", "pool"):
            for i in range(len(self.ops[e]) - 1, -1, -1):
                r = self.ops[e][i]
                if not r["dma"] and r["fn"] is not None:
                    last.append((e, i))
                    break
        for e in ENGS:
            self.op(e, None, extra=[d for d in last if d[0] != e] + list(extra))

    def emit(self, final_waits=()):
        nc = self.nc
        with contextlib.ExitStack() as st:
            csem = {e: st.enter_context(nc.semaphore("c_" + e)) for e in ENGS}
            dsem = {e: [st.enter_context(nc.semaphore("d_%s_%d" % (e, i))) for i in range(self.n_dma_sems)]
                    for e in ENGS if self.ndma[e] > 0}
            for e in ENGS:
                c = 0
                for r in self.ops[e]:
                    if r["signal"] and not r["dma"]:
                        c += 1
                    r["cval"] = c
            N = self.n_dma_sems

            def dma_sem_val(e2, i2):
                j = self.ops[e2][i2]["dma_j"]
                return e2, j % N, 16 * (j // N + 1)

            block = st.enter_context(nc.Block())

            def run(e, eng):
                for r in self.ops[e]:
                    for e2, i2 in r["waits"].items():
                        eng.wait_ge(csem[e2], self.ops[e2][i2]["cval"])
                    dw = {}
                    for (e2, i2) in r["dma_waits"]:
                        q, s, v = dma_sem_val(e2, i2)
                        dw[(q, s)] = max(dw.get((q, s), 0), v)
                    for (q, s), v in dw.items():
                        eng.wait_ge(dsem[q][s], v)
                    if r["fn"] is None:
                        continue
                    if r["dma"]:
                        j = r["dma_j"]
                        if j >= N:
                            eng.wait_ge(dsem[e][j % N], 16 * (j // N))
                    ins = r["fn"](eng)
                    if r["dma"]:
                        ins.then_inc(dsem[e][r["dma_j"] % N], 16)
                    elif r["signal"]:
                        ins.then_inc(csem[e], 1)
                if e == "sp":
                    dw = {}
                    for (e2, i2) in final_waits:
                        q, s, v = dma_sem_val(e2, i2)
                        dw[(q, s)] = max(dw.get((q, s), 0), v)
                    for (q, s), v in dw.items():
                        eng.wait_ge(dsem[q][s], v)

            @block.tensor
            def _(eng):
                run("pe", eng)

            @block.scalar
            def _(eng):
                run("act", eng)

            @block.vector
            def _(eng):
                run("dve", eng)

            @block.gpsimd
            def _(eng):
                run("pool", eng)

            @block.sync
            def _(eng):
                run("sp", eng)


O_G1, O_G3, O_BG, O_CW, O_CB, O_LG, O_LB, O_FW, O_FB, O_IF, O_SG = 0, 8, 16, 32, 280, 288, 296, 304, 436, 480, 481
NPRM = 482
NROW = 2064


def build_program():
    nc = bass.Bass("TRN2", target_bir_lowering=False)
    x_d = nc.dram_tensor("x", [8192, 1024], F32, kind="ExternalInput").ap()
    pos_d = nc.dram_tensor("pos", [4, 2048], I32, kind="ExternalInput").ap()
    win_d = nc.dram_tensor("w_in_p", [1024, 5376], F32, kind="ExternalInput").ap()
    wb_d = nc.dram_tensor("w_b_p", [1024, 2048], F32, kind="ExternalInput").ap()
    wout_d = nc.dram_tensor("w_out", [1024, 1024], F32, kind="ExternalInput").ap()
    wup_d = nc.dram_tensor("w_up_p", [1024, 5632], F32, kind="ExternalInput").ap()
    wdn_d = nc.dram_tensor("w_down", [2816, 1024], F32, kind="ExternalInput").ap()
    prm_d = nc.dram_tensor("prm", [128, NPRM], F32, kind="ExternalInput").ap()
    rowp_d = nc.dram_tensor("rowp", [1, NROW], F32, kind="ExternalInput").ap()
    y_d = nc.dram_tensor("y", [8192, 1024], F32, kind="ExternalOutput").ap()
    wA = nc.dram_tensor("wA", [11, 128, 8, 512], BF16).ap()
    wB = nc.dram_tensor("wB", [4, 128, 8, 512], BF16).ap()
    wC = nc.dram_tensor("wC", [2, 128, 8, 512], BF16).ap()
    wD = nc.dram_tensor("wD", [11, 128, 8, 512], BF16).ap()
    wE = nc.dram_tensor("wE", [2, 128, 22, 512], BF16).ap()
    wF = nc.dram_tensor("wF", [8, 128, 4096], BF16).ap()
    wG = nc.dram_tensor("wG", [5, 128, 4096], BF16).ap()

    def sb(name, shape, dt):
        return nc.alloc_sbuf_tensor(name, shape, dt).ap()

    xbuf = sb("xbuf", [128, 2, 4, 1024], F32)
    wring = sb("wring", [128, NWSLOT, 8, 512], BF16)
    wflat = wring.rearrange("p s k t -> p s (k t)")
    hTm = sb("hTm", [128, 4, 1024], BF16)
    hTa = sb("hTa", [128, 8, 512], BF16)
    hTb = sb("hTb", [128, 8, 512], BF16)
    kT = [sb("kT0", [128, 640], BF16), sb("kT1", [128, 640], BF16)]
    V = sb("V", [128, 5, 2, 66], BF16)
    uhalo = sb("uhalo", [128, 8, 30], BF16)
    rawhalo = sb("rawhalo", [128, 44, 2], BF16)
    prm = sb("prm_sb", [128, NPRM], F32)
    rowp = sb("rowp_sb", [128, NROW], F32)
    esink = sb("esink", [128, 16], F32)
    ident = sb("ident", [128, 128], BF16)
    ones = sb("ones", [128, 128], BF16)
    maskP = sb("maskP", [128, 4, 128], BF16)
    maskC = sb("maskC", [128, 4, 128], BF16)
    junk = sb("junk", [128, 1024], BF16)
    st = sb("st", [128, 64], F32)
    mhalf = sb("mhalf", [128, 8], F32)
    hb = sb("hb", [128, 16], F32)
    arena = sb("arena", [128, 51440], BF16)
    ps = nc.alloc_psum_tensor("ps", [128, 8, 512], F32).ap()

    def sub(off, n):
        return arena[:, off:off + n]

    uT = sub(0, 4336).rearrange("p (c t) -> p c t", c=8)
    ucT = sub(4336, 4096).rearrange("p (c t) -> p c t", c=8)
    usq = sub(8432, 1024).rearrange("p (c t) -> p c t", c=2)
    qT = sub(9456, 4096).rearrange("p (c t) -> p c t", c=8)
    gcT = sub(13552, 4096).rearrange("p (c t) -> p c t", c=8)
    gaT = sub(17648, 4096).rearrange("p (c t) -> p c t", c=8)
    PT = sub(21744, 4096).rearrange("p (s k t) -> p s k t", s=2, k=2)
    attn_tm = sub(25840, 2048).rearrange("p (s t) -> p s t", s=2)
    attnT = sub(27888, 4096).rearrange("p (c t) -> p c t", c=8)
    mergedT = sub(31984, 4096).rearrange("p (c t) -> p c t", c=8)
    diag = sub(36080, 4096).rearrange("p (s k t) -> p s k t", s=2, k=16)
    Rm = sub(36080, 128)
    posi = sub(40176, 1024).bitcast(I32)
    cosT = sub(41200, 1024).bitcast(F32)
    sinT = sub(42224, 1024).bitcast(F32)
    rt = [sub(43248, 1024).bitcast(F32), sub(44272, 1024).bitcast(F32)]
    tmpA = sub(45296, 2048).bitcast(F32).rearrange("p (s t) -> p s t", s=2)
    tmpB = sub(47344, 2048).bitcast(F32).rearrange("p (s t) -> p s t", s=2)
    sig = sub(49392, 2048).bitcast(F32).rearrange("p (s t) -> p s t", s=2)
    wEs = sub(0, 22528).rearrange("p (h k t) -> p h k t", h=2, k=22)
    zT = sub(22528, 11264).rearrange("p (c t) -> p c t", c=22)
    raw = sub(33792, 2064).rearrange("p (s t) -> p s t", s=4)
    sg = sub(35856, 2048).bitcast(F32).rearrange("p (s t) -> p s t", s=2)
    fdring = sub(39440, 7680).rearrange("p (s m t) -> p s m t", s=2, m=30)
    stin = sub(0, 16384).bitcast(F32).rearrange("p (s t) -> p s t", s=4)
    stout = sub(16384, 8192).rearrange("p (s t) -> p s t", s=4)
    identf = sub(24576, 1024).bitcast(F32)

    P = Prog(nc)
    R = P.region
    R("uTh", 0, 4336)
    for c in range(8):
        R("uT%d" % c, c * 542, (c + 1) * 542)
        R("uc%d" % c, 4336 + c * 512, 4336 + (c + 1) * 512)
        R("qT%d" % c, 9456 + c * 512, 9456 + (c + 1) * 512)
        R("g%d" % c, 13552 + c * 512, 13552 + (c + 1) * 512)
        R("g%d" % (8 + c), 17648 + c * 512, 17648 + (c + 1) * 512)
        R("mg%d" % c, 31984 + c * 512, 31984 + (c + 1) * 512)
    for sl in range(2):
        R("usq%d" % sl, 8432 + sl * 512, 8432 + (sl + 1) * 512)
        for kb in range(2):
            for half in range(2):
                o = 21744 + sl * 2048 + kb * 1024 + half * 512
                R("PT%d_%d_%d" % (sl, kb, half), o, o + 512)
        for g in range(2):
            for b2 in range(2):
                o = 25840 + sl * 1024 + (g * 8 + 4 * b2) * 64
                R("atm%d_%d_%d" % (sl, g, b2), o, o + 256)
        R("tA%d" % sl, 45296 + sl * 1024, 45296 + (sl + 1) * 1024)
        R("tB%d" % sl, 47344 + sl * 1024, 47344 + (sl + 1) * 1024)
        R("sig%d" % sl, 49392 + sl * 1024, 49392 + (sl + 1) * 1024)
        R("rt%d" % sl, 43248 + sl * 1024, 43248 + (sl + 1) * 1024)
        R("wE%d" % sl, sl * 11264, (sl + 1) * 11264)
        R("sg%d" % sl, 35856 + sl * 1024, 35856 + (sl + 1) * 1024)
        R("fdr%d" % sl, 39440 + sl * 3840, 39440 + (sl + 1) * 3840)
    for n in range(4):
        R("aT%d" % n, 27888, 31984)
        R("rawh%d" % n, 33792 + n * 516, 33792 + n * 516 + 2)
        R("raw%d" % n, 33792 + n * 516 + 2, 33792 + (n + 1) * 516)
    R("fx0", 39440, 40464)
    R("fx1", 40464, 41488)
    R("Rm", 36080, 36208)
    R("posi", 40176, 41200)
    R("cosT", 41200, 42224)
    R("sinT", 42224, 43248)
    for j in range(22):
        R("z%d" % j, 22528 + j * 512, 22528 + (j + 1) * 512)
    R("identf", 24576, 25600)
    for sl in range(4):
        R("stin%d" % sl, sl * 4096, (sl + 1) * 4096)
        R("stout%d" % sl, 16384 + sl * 2048, 16384 + (sl + 1) * 2048)

    def DMA(q, out, in_, reads=(), writes=(), extra=()):
        return P.op(q, lambda e: e.dma_start(out=out, in_=in_), reads, writes, dma=True, extra=extra)

    def ACT(out, in_, func, reads, writes, scale=1.0, bias=0.0, accum=None):
        def fn(e):
            if accum is not None:
                return e.activation(out=out, in_=in_, func=func, scale=scale, bias=bias, accum_out=accum)
            return e.activation(out=out, in_=in_, func=func, scale=scale, bias=bias)
        return P.op("act", fn, reads, writes)

    def TS(eng, out, in0, s1, s2, op0, op1, reads, writes):
        def fn(e):
            if s2 is None:
                return e.tensor_scalar(out=out, in0=in0, scalar1=s1, scalar2=None, op0=op0)
            return e.tensor_scalar(out=out, in0=in0, scalar1=s1, scalar2=s2, op0=op0, op1=op1)
        return P.op(eng, fn, reads, writes)

    def STT(out, in0, scalar, in1, op0, op1, reads, writes):
        return P.op("dve", lambda e: e.scalar_tensor_tensor(out=out, in0=in0, scalar=scalar, in1=in1, op0=op0, op1=op1),
                    reads, writes)

    def TT(eng, out, in0, in1, op, reads, writes):
        return P.op(eng, lambda e: e.tensor_tensor(out=out, in0=in0, in1=in1, op=op), reads, writes)

    def CP(eng, out, in_, reads, writes):
        return P.op(eng, lambda e: e.tensor_copy(out=out, in_=in_), reads, writes)

    def MEMSET(eng, ap, val, writes):
        return P.op(eng, lambda e: e.memset(ap, val), (), writes)

    def MM(out, pairs, reads, writes, start=True, stop=True):
        def fn(e):
            n = len(pairs)
            ins = None
            for i, (l, r) in enumerate(pairs):
                ins = e.matmul(out, lhsT=l, rhs=r, start=(start and i == 0), stop=(stop and i == n - 1))
            return ins
        return P.op("pe", fn, reads, writes)

    def TRN(outs_ins, reads, writes):
        def fn(e):
            ins = None
            for (o, i_) in outs_ins:
                ins = e.transpose(out=o, in_=i_, identity=ident)
            return ins
        return P.op("pe", fn, reads, writes)

    bank_ctr = [0]

    def next_bank():
        b = bank_ctr[0] % 4
        bank_ctr[0] += 1
        return b

    wctr = [0]

    def wload(src):
        slot = wctr[0] % NWSLOT
        wctr[0] += 1
        w = src.shape[-1]
        DMA("sp", wring[:, slot, :, 0:w], src, (), ["wr%d" % slot])
        return slot

    def wload_flat(src):
        slot = wctr[0] % NWSLOT
        wctr[0] += 1
        DMA("sp", wflat[:, slot, 0:src.shape[-1]], src, (), ["wr%d" % slot])
        return slot

    def dgview(slot):
        return wflat[:, slot, :].rearrange("p (m t) -> p m t", m=32)

    def pcol(off, j=0):
        return prm[:, off + j:off + j + 1]

    DMA("sp", prm, prm_d, (), ["prm"])
    DMA("sp", rowp, rowp_d.partition_broadcast(128), (), ["rowp"])
    P.op("pool", lambda e: e.iota(identf, pattern=[[0, 4], [1, 128]], base=0, channel_multiplier=-1,
                                  allow_small_or_imprecise_dtypes=True), (), ["identf"])
    TS("dve", ident, identf[:, 0:128], 0.0, None, ALU.is_equal, None, ["identf"], ["ident"])
    NEG = -30000.0
    TS("dve", maskC.rearrange("p a b -> p (a b)"), identf, 0.0, NEG, ALU.is_lt, ALU.mult, ["identf"], ["maskC"])
    TS("dve", maskP.rearrange("p a b -> p (a b)"), identf, 0.0, NEG, ALU.is_ge, ALU.mult, ["identf"], ["maskP"])
    MEMSET("dve", ones, 1.0, ["ones"])
    MEMSET("dve", mhalf, -0.5, ["mhalf"])
    MEMSET("dve", kT[0], 0.0, ["kT"])
    MEMSET("dve", kT[1], 0.0, ["kT"])
    MEMSET("dve", V.rearrange("p a b c -> p (a b c)"), 1.0, ["V"])
    ACT(esink, rowp[:, 2048:2064], AF.Exp, ["rowp"], ["esink"])
    TS("dve", hb, prm[:, O_BG:O_BG + 16], 0.5, None, ALU.mult, None, ["prm"], ["hb"])
    TS("dve", rowp[:, 0:1024], rowp[:, 0:1024], 0.5, None, ALU.mult, None, ["rowp"], ["rowp"])

    stores = []
    pc = [0]

    def prep(src, dst, KC, C, gain_off):
        for kc in range(KC):
            c0 = 0
            while c0 < C:
                cw = min(2048, C - c0)
                if cw > 512 and cw % 512:
                    cw = (cw // 512) * 512
                slot = pc[0] % 4
                DMA("sp", stin[:, slot, 0:cw], src[kc * 128:(kc + 1) * 128, c0:c0 + cw], (), ["stin%d" % slot])
                eng = "act" if pc[0] % 2 == 0 else "dve"
                o_ap = stout[:, slot, 0:cw]
                i_ap = stin[:, slot, 0:cw]
                rd = ["stin%d" % slot, "prm"]
                wr = ["stout%d" % slot]
                if gain_off is None:
                    if eng == "act":
                        ACT(o_ap, i_ap, AF.Copy, rd, wr)
                    else:
                        CP("dve", o_ap, i_ap, rd, wr)
                else:
                    g = pcol(gain_off, kc)
                    if eng == "act":
                        ACT(o_ap, i_ap, AF.Identity, rd, wr, scale=g)
                    else:
                        TS("dve", o_ap, i_ap, g, None, ALU.mult, None, rd, wr)
                u0 = c0 // 512
                w = min(512, cw)
                nu = cw // w
                d_ap = dst[u0:u0 + nu, :, kc, 0:w].rearrange("u p c -> p u c")
                s_ap = stout[:, slot, 0:cw].rearrange("p (u c) -> p u c", u=nu)
                stores.append(DMA("act", d_ap, s_ap, ["stout%d" % slot], ()))
                pc[0] += 1
                c0 += cw

    stbf = sub(0, 16384).rearrange("p (s m t) -> p s m t", s=4, m=32)

    def build_diags(dst, cols):
        slot = pc[0] % 4
        pc[0] += 1

        def fn(e):
            ins = None
            for m, col in enumerate(cols):
                ins = e.tensor_scalar(out=stbf[:, slot, m, :], in0=ident, scalar1=prm[:, col:col + 1],
                                      scalar2=None, op0=ALU.mult)
            return ins
        P.op("dve", fn, ["ident", "prm"], ["stin%d" % slot])
        n = len(cols)
        stores.append(DMA("act", dst[:, 0:n * 128], stbf[:, slot, 0:n, :].rearrange("p m t -> p (m t)"),
                          ["stin%d" % slot], ()))

    prep(win_d, wA, 8, 5376, O_G1)
    for c in range(8):
        build_diags(wF[c], [O_CW + c * 31 + k for k in range(31)])
    prep(wb_d, wB, 8, 2048, None)
    prep(wout_d, wC, 8, 1024, None)
    for u in range(5):
        chs = range(10 * u, min(44, 10 * u + 10))
        build_diags(wG[u], [O_FW + ch * 3 + k for ch in chs for k in range(3)])
    prep(wup_d, wD, 8, 5632, O_G3)
    prep(wdn_d, wE, 22, 1024, None)

    def load_x(i):
        b = i % 2
        t0 = i * 512
        DMA("sp", xbuf[:, b, :, :], x_d[t0:t0 + 512, :].rearrange("(s p) d -> p s d", p=128),
            (), ["xb%ds%d" % (b, s) for s in range(4)])

    def pre_stats(b, s):
        ACT(junk, xbuf[:, b, s, :], AF.Square, ["xb%ds%d" % (b, s)], ["junk", "ssq%d" % s], accum=st[:, s:s + 1])
        TS("dve", st[:, 4 + s:5 + s], st[:, s:s + 1], 1.0 / 1024, 1e-6, ALU.mult, ALU.add, ["ssq%d" % s], ["ms%d" % s])
        TT("pool", st[:, 8 + s:9 + s], st[:, 4 + s:5 + s], mhalf[:, 0:1], ALU.pow, ["ms%d" % s, "mhalf"], ["rstd%d" % s])

    def pre_h(b, s):
        TS("dve", hTm[:, s, :], xbuf[:, b, s, :], st[:, 8 + s:9 + s], None, ALU.mult, None,
           ["xb%ds%d" % (b, s), "rstd%d" % s], ["hTm%d" % s])

    def pre_TR(s, which):
        hT = hTa if which == "a" else hTb
        tb = 4 if s % 2 == 0 else 7
        pst = ps[:, tb, :].bitcast(BF16)
        TRN([(pst[:, kc * 128:(kc + 1) * 128], hTm[:, s, kc * 128:(kc + 1) * 128]) for kc in range(8)],
            ["hTm%d" % s, "ident"], ["ps%d" % tb])
        if s % 2:
            CP("dve", hT[:, :, s * 128:(s + 1) * 128], pst.rearrange("p (k t) -> p k t", k=8), ["ps%d" % tb],
               ["hT%s%d" % (which, s)])
        else:
            ACT(hT[:, :, s * 128:(s + 1) * 128], pst.rearrange("p (k t) -> p k t", k=8), AF.Copy, ["ps%d" % tb],
                ["hT%s%d" % (which, s)])

    def pre_T(b, s, which):
        pre_h(b, s)
        pre_TR(s, which)

    HTA = ["hTa%d" % s for s in range(4)]
    HTB = ["hTb%d" % s for s in range(4)]

    def rope_tables(i):
        seq, ti = i // 4, i % 4
        DMA("sp", posi, pos_d[seq:seq + 1, ti * 512:(ti + 1) * 512].partition_broadcast(128), (), ["posi"])
        ang = tmpA[:, 0, :]
        a2 = tmpA[:, 1, :]
        kf = tmpB[:, 0, :]
        ki = tmpB[:, 1, :].bitcast(I32)
        C1 = 6.28125
        C2 = float(2 * np.pi - 6.28125)
        CL = 3.1415925
        CP("dve", ang, posi, ["posi"], ["tA0"])
        TS("dve", ang, ang, pcol(O_IF), None, ALU.mult, None, ["tA0", "prm"], ["tA0"])
        for (tab, shift, key) in ((sinT, 0.0, "sinT"), (cosT, PI / 2, "cosT")):
            TS("dve", a2, ang, shift, None, ALU.add, None, ["tA0"], ["tA1"])
            TS("dve", kf, a2, float(1.0 / (2 * np.pi)), None, ALU.mult, None, ["tA1"], ["tB0"])
            CP("dve", ki, kf, ["tB0"], ["tB1"])
            CP("dve", kf, ki, ["tB1"], ["tB0"])
            STT(a2, kf, -C1, a2, ALU.mult, ALU.add, ["tB0", "tA1"], ["tA1"])
            STT(a2, kf, -C2, a2, ALU.mult, ALU.add, ["tB0", "tA1"], ["tA1"])
            TS("dve", a2, a2, CL, -CL, ALU.min, ALU.max, ["tA1"], ["tA1"])
            if key == "sinT":
                ACT(tab, a2, AF.Sin, ["tA1", "prm"], [key], scale=pcol(O_SG))
            else:
                ACT(tab, a2, AF.Sin, ["tA1"], [key])

    def proj_chunk(slot, j, which="a"):
        b = next_bank()
        hT = hTa if which == "a" else hTb
        MM(ps[:, b, :], [(wring[:, slot, kc, j * 128:(j + 1) * 128], hT[:, kc, :]) for kc in range(8)],
           ["wr%d" % slot] + (HTA if which == "a" else HTB), ["ps%d" % b])
        return b

    def phase_A1(i):
        first = (i % 4 == 0)
        if not first:
            for g in range(2):
                CP("pool", kT[g][:, 0:128], kT[g][:, 512:640], ["kT"], ["kT"])
            CP("pool", V[:, 0, :, 0:64], V[:, 4, :, 0:64], ["V"], ["V"])
            CP("pool", uT[:, :, 0:30], uhalo, ["uhalo"], ["uTh"])
        else:
            MEMSET("pool", uT[:, :, 0:30], 0.0, ["uTh"])
        for u in range(4):
            slot = wload(wA[u])
            for half in range(2):
                c = 2 * u + half
                bv = proj_chunk(slot, 2 * half)
                bg = proj_chunk(slot, 2 * half + 1)
                sl = c % 2
                ACT(sig[:, sl, :], ps[:, bg, :], AF.Tanh, ["ps%d" % bg], ["sig%d" % sl], scale=0.5)
                STT(uT[:, c, 30:542], sig[:, sl, :], 1.0, ps[:, bv, :], ALU.add, ALU.mult,
                    ["ps%d" % bv, "sig%d" % sl], ["uT%d" % c])

    def phase_A2(i):
        for (d0, s0) in ((0, 32), (32, 0), (64, 96), (96, 64)):
            CP("dve", Rm[:, d0:d0 + 32], ident[:, s0:s0 + 32], ["ident"], ["Rm"])

        def rope_a(bq, sl):
            ACT(usq[:, sl, :], ps[:, bq, :], AF.Copy, ["ps%d" % bq], ["usq%d" % sl])

        def rope_b(bq, sl, r0, r1, k0, k1, fin):
            br = next_bank()
            MM(ps[:, br, :], [(Rm, usq[:, sl, :])], ["Rm", "usq%d" % sl], ["ps%d" % br])
            TT("dve", r0, ps[:, bq, :], cosT, ALU.mult, ["ps%d" % bq, "cosT"], [k0])
            TT("dve", r1, ps[:, br, :], sinT, ALU.mult, ["ps%d" % br, "sinT"], [k1])
            fin()

        pend = [None]

        def flush():
            if pend[0] is not None:
                rope_b(*pend[0])
                pend[0] = None

        for u in range(2):
            slot = wload(wA[4 + u])
            for j in range(4):
                c = 4 * u + j
                bq = proj_chunk(slot, j)
                rope_a(bq, c % 2)
                flush()
                if c % 2 == 0:
                    r0, r1, k0, k1 = rt[0], rt[1], "rt0", "rt1"
                else:
                    r0, r1, k0, k1 = sig[:, 0, :], sig[:, 1, :], "sig0", "sig1"

                def fin(c=c, r0=r0, r1=r1, k0=k0, k1=k1):
                    TT("pool", qT[:, c, :], r0, r1, ALU.add, [k0, k1], ["qT%d" % c])
                    ln_apply(c)
                pend[0] = (bq, c % 2, r0, r1, k0, k1, fin)
        slot = wload(wA[10][:, :, 0:256])
        bq = proj_chunk(slot, 0)
        rope_a(bq, 0)
        flush()

        def fink():
            TT("pool", kT[0][0:64, 128:640], rt[0][0:64, :], rt[1][0:64, :], ALU.add, ["rt0", "rt1"], ["kT"])
            TT("pool", kT[1][64:128, 128:640], rt[0][64:128, :], rt[1][64:128, :], ALU.add, ["rt0", "rt1"], ["kT"])
        pend[0] = (bq, 0, rt[0], rt[1], "rt0", "rt1", fink)
        for s in range(4):
            MM(ps[:, 7, s * 128:(s + 1) * 128],
               [(hTa[:, kc, s * 128:(s + 1) * 128], wring[:, slot, kc, 128:256]) for kc in range(8)],
               ["wr%d" % slot] + HTA, ["ps7"])
        ACT(V[:, 1:5, :, 0:64], ps[:, 7, :].rearrange("p (s g d) -> p s g d", s=4, g=2), AF.Copy, ["ps7"], ["V"])
        flush()
        for u in range(4):
            slot = wload(wA[6 + u])
            for j in range(4):
                cc = 4 * u + j
                b = proj_chunk(slot, j)
                dst = gcT[:, cc, :] if cc < 8 else gaT[:, cc - 8, :]
                ACT(dst, ps[:, b, :], AF.Tanh, ["ps%d" % b, "hb"], ["g%d" % cc], scale=0.5, bias=hb[:, cc:cc + 1])

    dctr = [0]

    def phase_B(i):
        CP("pool", uhalo, uT[:, :, 512:542], ["uT%d" % c for c in range(8)], ["uhalo"])

        def stats(c):
            sl = c % 2
            MM(ps[:, 5, :], [(ones, ucT[:, c, :])], ["ones", "uc%d" % c], ["ps5"], start=(c == 0), stop=(c == 7))
            MM(ps[:, 6, :], [(ones, usq[:, sl, :])], ["ones", "usq%d" % sl], ["ps6"], start=(c == 0), stop=(c == 7))

        for c in range(8):
            b = next_bank()
            dslot = wload_flat(wF[c][:, 0:31 * 128])
            dv = dgview(dslot)
            MM(ps[:, b, :], [(dv[:, k, :], uT[:, c, k:k + 512]) for k in range(31)],
               ["wr%d" % dslot, "uT%d" % c, "uTh"], ["ps%d" % b])
            sl = c % 2
            ACT(ucT[:, c, :], ps[:, b, :], AF.Identity, ["ps%d" % b, "prm"], ["uc%d" % c], scale=0.5, bias=pcol(O_CB, c))
            ACT(usq[:, sl, :], ps[:, b, :], AF.Square, ["ps%d" % b, "prm"], ["usq%d" % sl], scale=0.5, bias=pcol(O_CB, c))
            if c >= 1:
                stats(c - 1)
        stats(7)

    def ln_head():
        mean = tmpA[:, 0, :]
        m2 = tmpA[:, 1, :]
        var = tmpB[:, 0, :]
        TS("dve", mean, ps[:, 5, :], 1.0 / 1024, None, ALU.mult, None, ["ps5"], ["tA0"])
        ACT(m2, mean, AF.Square, ["tA0"], ["tA1"])
        STT(var, ps[:, 6, :], 1.0 / 1024, m2, ALU.mult, ALU.subtract, ["ps6", "tA1"], ["tB0"])
        TS("dve", var, var, 1e-5, None, ALU.add, None, ["tB0"], ["tB0"])
        ACT(var, var, AF.Sqrt, ["tB0"], ["tB0"])
        P.op("dve", lambda e: e.reciprocal(out=ps[:, 5, :], in_=var), ["tB0"], ["ps5"])
        STT(ps[:, 6, :], mean, -1.0, ps[:, 5, :], ALU.mult, ALU.mult, ["tA0", "ps5"], ["ps6"])

    def ln_apply(c):
        sl = c % 2
        tk = "tB1" if sl else "tA1"
        tb = tmpB[:, 1, :] if sl else tmpA[:, 1, :]
        TT("dve", tb, ucT[:, c, :], ps[:, 5, :], ALU.mult, ["uc%d" % c, "ps5"], [tk])
        TT("dve", tb, tb, ps[:, 6, :], ALU.add, [tk, "ps6"], [tk])
        ACT(ucT[:, c, :], tb, AF.Silu, [tk, "prm"], ["uc%d" % c], scale=pcol(O_LG, c), bias=pcol(O_LB, c))

    sbank = [0]

    def phase_C(i):
        first = (i % 4 == 0)
        OB = (3, 7)

        def unit_S(k):
            n, g = k // 2, k % 2
            has_prev = not (first and n == 0)
            kbs = ([0] if has_prev else []) + [1]
            psl = k % 2
            for kb in kbs:
                kcols = slice(128 * (n + kb), 128 * (n + kb) + 128)
                msk = maskP if kb == 0 else maskC
                for half in range(2):
                    b = sbank[0] % 3
                    sbank[0] += 1
                    MM(ps[:, b, :], [(kT[g][:, kcols], qT[:, 4 * half:4 * half + 4, 128 * n:128 * n + 128]),
                                     (ident, msk)],
                       ["kT", "ident", "maskP", "maskC"] + ["qT%d" % c for c in range(4 * half, 4 * half + 4)],
                       ["ps%d" % b])
                    ACT(PT[:, psl, kb, half * 512:(half + 1) * 512], ps[:, b, :], AF.Exp, ["ps%d" % b],
                        ["PT%d_%d_%d" % (psl, kb, half)], scale=0.125)

        def unit_PV(k):
            n, g = k // 2, k % 2
            has_prev = not (first and n == 0)
            kbs = ([0] if has_prev else []) + [1]
            psl = k % 2
            asl = n % 2
            for bank2 in range(2):
                ob = OB[bank2]
                for cc in range(4):
                    c = 4 * bank2 + cc
                    MM(ps[:, ob, cc * 65:cc * 65 + 65],
                       [(PT[:, psl, kb, c * 128:(c + 1) * 128], V[:, n + kb, g, 0:65]) for kb in kbs],
                       ["V"] + ["PT%d_%d_%d" % (psl, kb, bank2) for kb in kbs], ["ps%d" % ob])
                h0 = g * 8 + 4 * bank2
                o3 = ps[:, ob, 0:260].rearrange("p (c d) -> p c d", c=4)
                den = st[:, 16 + 4 * bank2:20 + 4 * bank2]
                TT("dve", den, o3[:, :, 64], esink[:, h0:h0 + 4], ALU.add, ["ps%d" % ob, "esink"], ["den%d" % bank2])
                P.op("dve", lambda e, den=den: e.reciprocal(out=den, in_=den), ["den%d" % bank2], ["den%d" % bank2])

                def nfn(e, bank2=bank2, h0=h0, o3=o3, asl=asl):
                    ins = None
                    for cc in range(4):
                        h = h0 + cc
                        ins = e.tensor_scalar(out=attn_tm[:, asl, h * 64:(h + 1) * 64], in0=o3[:, cc, 0:64],
                                              scalar1=st[:, 16 + 4 * bank2 + cc:17 + 4 * bank2 + cc],
                                              scalar2=None, op0=ALU.mult)
                    return ins
                P.op("dve", nfn, ["ps%d" % ob, "den%d" % bank2], ["atm%d_%d_%d" % (asl, g, bank2)])

        def unit_T(n):
            asl = n % 2
            pst = ps[:, 4, :].bitcast(BF16)
            TRN([(pst[:, kc * 128:(kc + 1) * 128], attn_tm[:, asl, kc * 128:(kc + 1) * 128]) for kc in range(8)],
                ["atm%d_%d_%d" % (asl, g, b2) for g in range(2) for b2 in range(2)] + ["ident"], ["ps4"])
            ACT(attnT[:, :, n * 128:(n + 1) * 128], pst.rearrange("p (k t) -> p k t", k=8), AF.Copy, ["ps4"], ["aT%d" % n])

        unit_S(0)
        for k in range(8):
            if k + 1 < 8:
                unit_S(k + 1)
            unit_PV(k)
            if k >= 2 and k % 2 == 0:
                unit_T(k // 2 - 1)
        unit_T(3)

    def post_norm(i, wslots_or_views, nk, lhs_buf, lhs_keys, goff, tms, hook_a=None, hook_pe=None):
        b = i % 2
        for s in range(4):
            mine = tms[2 * (s % 2):2 * (s % 2) + 2]
            for half in range(2):
                bk = next_bank()
                tm, tk = mine[half]
                MM(ps[:, bk, :], [(lhs_buf[:, kc, s * 128:(s + 1) * 128], wslots_or_views[half][0][:, kc, :]) for kc in range(nk)],
                   lhs_keys + [wslots_or_views[half][1]], ["ps%d" % bk])
                ACT(junk[:, 0:512], ps[:, bk, :], AF.Square, ["ps%d" % bk], ["junk", "pq%d" % half],
                    accum=st[:, 24 + half:25 + half])
                TT("dve", tm, ps[:, bk, :], rowp[:, goff + half * 512:goff + (half + 1) * 512], ALU.mult,
                   ["ps%d" % bk, "rowp"], [tk])
            TT("dve", st[:, 26:27], st[:, 24:25], st[:, 25:26], ALU.add, ["pq0", "pq1"], ["pms"])
            TS("dve", st[:, 26:27], st[:, 26:27], (1.0 / 4096) if goff == 0 else (1.0 / 1024), 1e-6, ALU.mult, ALU.add,
               ["pms"], ["pms"])
            TT("pool", st[:, 28 + s:29 + s], st[:, 26:27], mhalf[:, 0:1], ALU.pow, ["pms", "mhalf"], ["prs%d" % s])
            for half in range(2):
                tm, tk = mine[half]
                xk = "xb%ds%d" % (b, s)
                STT(xbuf[:, b, s, half * 512:(half + 1) * 512], tm, st[:, 28 + s:29 + s],
                    xbuf[:, b, s, half * 512:(half + 1) * 512], ALU.mult, ALU.add, [tk, xk, "prs%d" % s], [xk])
            if hook_a is not None:
                hook_a(s)
            if hook_pe is not None and s >= 1:
                hook_pe[0](s - 1)
        if hook_pe is not None:
            hook_pe[0](3)
            for s in range(4):
                hook_pe[1](s)

    def phase_D(i):
        for u in range(4):
            slot = wload(wB[u])
            for half in range(2):
                c = 2 * u + half
                ba = next_bank()
                MM(ps[:, ba, :], [(wring[:, slot, kc, (2 * half) * 128:(2 * half + 1) * 128], ucT[:, kc, :]) for kc in range(8)],
                   ["wr%d" % slot] + ["uc%d" % k for k in range(8)], ["ps%d" % ba])
                bb = next_bank()
                MM(ps[:, bb, :], [(wring[:, slot, kc, (2 * half + 1) * 128:(2 * half + 2) * 128], attnT[:, kc, :]) for kc in range(8)],
                   ["wr%d" % slot] + ["aT%d" % n for n in range(4)], ["ps%d" % bb])
                sl = c % 2
                STT(tmpA[:, sl, :], gcT[:, c, :], 1.0, ps[:, ba, :], ALU.add, ALU.mult, ["ps%d" % ba, "g%d" % c], ["tA%d" % sl])
                STT(tmpB[:, sl, :], gaT[:, c, :], 1.0, ps[:, bb, :], ALU.add, ALU.mult, ["ps%d" % bb, "g%d" % (8 + c)], ["tB%d" % sl])
                TT("pool", mergedT[:, c, :], tmpA[:, sl, :], tmpB[:, sl, :], ALU.add, ["tA%d" % sl, "tB%d" % sl], ["mg%d" % c])
        s0 = wload(wC[0])
        s1 = wload(wC[1])
        b = i % 2
        post_norm(i, [(wring[:, s0, :, :], "wr%d" % s0), (wring[:, s1, :, :], "wr%d" % s1)], 8, mergedT,
                  ["mg%d" % c for c in range(8)], 0,
                  [(tmpA[:, 0, :], "tA0"), (tmpA[:, 1, :], "tA1"), (tmpB[:, 0, :], "tB0"), (tmpB[:, 1, :], "tB1")],
                  hook_a=lambda s: pre_stats(b, s), hook_pe=(lambda s: pre_h(b, s), lambda s: pre_TR(s, "b")))

    rctr = [0]

    def phase_E(i):
        first = (i % 4 == 0)
        b = i % 2
        if i + 1 < NT:
            load_x(i + 1)
        pend = [None]

        def conv_part(rs, ch, which, j2, sgl, fslot):
            bc = next_bank()
            MM(ps[:, bc, :], [(fdring[:, fslot, (ch % 10) * 3 + k, :], raw[:, rs, k:k + 512]) for k in range(3)],
               ["raw%d" % rs, "rawh%d" % rs, "fdr%d" % fslot], ["ps%d" % bc])
            if which == 0:
                ACT(sg[:, sgl, :], ps[:, bc, :], AF.Silu, ["ps%d" % bc, "prm"], ["sg%d" % sgl], bias=pcol(O_FB, ch))
            else:
                STT(zT[:, j2, :], ps[:, bc, :], pcol(O_FB, ch), sg[:, sgl, :], ALU.add, ALU.mult,
                    ["ps%d" % bc, "sg%d" % sgl, "prm"], ["z%d" % j2])

        def fload(uu):
            nch = min(44, 10 * uu + 10) - 10 * uu
            fs = uu % 2
            DMA("sp", fdring[:, fs, 0:nch * 3, :], wG[uu][:, 0:nch * 3 * 128].rearrange("p (m t) -> p m t", t=128),
                (), ["fdr%d" % fs])
            return fs

        fsl = [fload(0)]
        fnext = [None]
        for u in range(11):
            slot = wload(wD[u])
            if u == 2:
                for hh in range(2):
                    DMA("sp", wEs[:, hh, :, :], wE[hh], (), ["wE%d" % hh])
            if u == 4 and i + 1 < NT:
                for s in range(4):
                    pre_stats((i + 1) % 2, s)
            if u in (5, 6, 7, 8) and i + 1 < NT:
                pre_T((i + 1) % 2, u - 5, "a")
            for half in range(2):
                j2 = 2 * u + half
                sgl = j2 % 2
                for which in range(2):
                    ch = 2 * j2 + which
                    bk = proj_chunk(slot, 2 * half + which, "b")
                    rs = rctr[0] % 4
                    rctr[0] += 1
                    ACT(raw[:, rs, 2:514], ps[:, bk, :], AF.Copy, ["ps%d" % bk], ["raw%d" % rs])

                    if ch % 10 == 0 and ch > 0:
                        fsl[0] = fnext[0]
                    if (ch + 4) % 10 == 0 and ch + 4 < 44:
                        fnext[0] = fload((ch + 4) // 10)
                    if first:
                        MEMSET("pool", raw[:, rs, 0:2], 0.0, ["rawh%d" % rs])
                    else:
                        CP("pool", raw[:, rs, 0:2], rawhalo[:, ch, :], ["rh%d" % ch], ["rawh%d" % rs])
                    CP("pool", rawhalo[:, ch, :], raw[:, rs, 512:514], ["raw%d" % rs], ["rh%d" % ch])
                    if pend[0] is not None:
                        conv_part(*pend[0])
                    pend[0] = (rs, ch, which, j2, sgl, fsl[0])
        conv_part(*pend[0])
        t0 = i * 512
        outs = []

        def store(s):
            outs.append(DMA("act", y_d[t0 + s * 128:t0 + (s + 1) * 128, :], xbuf[:, b, s, :], ["xb%ds%d" % (b, s)], ()))
        post_norm(i, [(wEs[:, 0, :, :], "wE0"), (wEs[:, 1, :, :], "wE1")], 22, zT,
                  ["z%d" % j for j in range(22)], 1024,
                  [(sg[:, 0, :], "sg0"), (sg[:, 1, :], "sg1"),
                   (sub(39440, 1024).bitcast(F32), "fx0"), (sub(40464, 1024).bitcast(F32), "fx1")],
                  hook_a=store)
        return outs

    load_x(0)
    P.fence(extra=stores)
    final = []
    for s in range(4):
        pre_stats(0, s)
    for s in range(4):
        pre_T(0, s, "a")
    for i in range(NT):
        phase_A1(i)
        rope_tables(i)
        phase_B(i)
        ln_head()
        phase_A2(i)
        phase_C(i)
        phase_D(i)
        final += phase_E(i)
    P.emit(final_waits=final)
    return nc


def _host_layout(inputs):
    f = np.float32
    w_in = np.asarray(inputs["w_in"], f)[0]
    cols = []
    for c in range(8):
        cols += list(range(c * 128, (c + 1) * 128)) + list(range(1024 + c * 128, 1024 + (c + 1) * 128))
    d = np.arange(64)
    for c in range(8):
        for h in (c, 8 + c):
            cols += list(2048 + h * 64 + d)
    cols += list(range(3328, 5376))
    cols += list(range(3072, 3200))
    cols += list(range(3200, 3328))
    w_in_p = np.ascontiguousarray(w_in[:, np.array(cols)])
    wco = np.asarray(inputs["w_conv_out"], f)[0]
    wao = np.asarray(inputs["w_attn_out"], f)[0]
    w_b_p = np.ascontiguousarray(np.concatenate(
        [m[:, c * 128:(c + 1) * 128] for c in range(8) for m in (wco, wao)], axis=1))
    w_up = np.asarray(inputs["w_up"], f)[0]
    upcols = []
    for j in range(22):
        upcols += list(range(j * 128, (j + 1) * 128)) + list(range(2816 + j * 128, 2816 + (j + 1) * 128))
    upcols = np.array(upcols)
    w_up_p = np.ascontiguousarray(w_up[:, upcols])
    prm = np.zeros((128, NPRM), f)
    prm[:, O_G1:O_G1 + 8] = np.asarray(inputs["ln_mix_pre"], f)[0].reshape(8, 128).T
    prm[:, O_G3:O_G3 + 8] = np.asarray(inputs["ln_ffn_pre"], f)[0].reshape(8, 128).T
    prm[:, O_BG:O_BG + 16] = np.asarray(inputs["b_gate"], f)[0].reshape(16, 128).T
    cw = np.asarray(inputs["conv_dw_w"], f)[0]
    prm[:, O_CW:O_CW + 248] = cw.T.reshape(8, 128, 31).transpose(1, 0, 2).reshape(128, 248)
    prm[:, O_CB:O_CB + 8] = np.asarray(inputs["conv_dw_b"], f)[0].reshape(8, 128).T
    prm[:, O_LG:O_LG + 8] = np.asarray(inputs["conv_ln_g"], f)[0].reshape(8, 128).T
    prm[:, O_LB:O_LB + 8] = np.asarray(inputs["conv_ln_b"], f)[0].reshape(8, 128).T
    fw = np.asarray(inputs["ffn_dw_w"], f)[0][:, upcols]
    prm[:, O_FW:O_FW + 132] = fw.T.reshape(44, 128, 3).transpose(1, 0, 2).reshape(128, 132)
    prm[:, O_FB:O_FB + 44] = np.asarray(inputs["ffn_dw_b"], f)[0][upcols].reshape(44, 128).T
    p = np.arange(128)
    inv_freq = (10000.0 ** (-(np.arange(0, 64, 2, dtype=np.float32)) / np.float32(64))).astype(f)
    prm[:, O_IF] = inv_freq[p % 32]
    prm[:, O_SG] = np.where((p % 64) < 32, -1.0, 1.0)
    rowp = np.zeros((1, NROW), f)
    rowp[0, 0:1024] = np.asarray(inputs["ln_mix_post"], f)[0]
    rowp[0, 1024:2048] = np.asarray(inputs["ln_ffn_post"], f)[0]
    rowp[0, 2048:2064] = np.asarray(inputs["attn_sinks"], f)[0]
    shared = {
        "w_in_p": w_in_p, "w_b_p": w_b_p, "w_out": np.ascontiguousarray(np.asarray(inputs["w_out"], f)[0]),
        "w_up_p": w_up_p, "w_down": np.ascontiguousarray(np.asarray(inputs["w_down"], f)[0]),
        "prm": prm, "rowp": rowp,
    }
    x = np.asarray(inputs["x"], f)
    pos = np.asarray(inputs["positions"], np.int32)
    in_maps = []
    for c in range(NCORES):
        m = dict(shared)
        m["x"] = np.ascontiguousarray(x[4 * c:4 * c + 4].reshape(8192, 1024))
        m["pos"] = np.ascontiguousarray(pos[4 * c:4 * c + 4])
        in_maps.append(m)
    return in_maps


_NC_CACHE = {}


def kernel(**inputs):
    in_maps = _host_layout(inputs)
    if "nc" not in _NC_CACHE:
        _NC_CACHE["nc"] = build_program()
    nc = _NC_CACHE["nc"]
    res = run_bass_kernel_spmd(nc, in_maps, core_ids=list(range(NCORES)))
    out = np.concatenate([np.asarray(r["y"]).reshape(4, 2048, 1024) for r in res.results], axis=0)
    return out.astype(np.float32)
```

```python
import contextlib
import numpy as np
import concourse.bass as bass
import concourse.mybir as mybir
from concourse.bass_utils import run_bass_kernel_spmd

F32 = mybir.dt.float32
BF16 = mybir.dt.bfloat16
I32 = mybir.dt.int32
AF = mybir.ActivationFunctionType
ALU = mybir.AluOpType

ENGS = ["pe", "act", "dve", "pool", "sp"]
NCORES = 8
NT = 16
NWSLOT = 4
PI = float(np.pi)


class Prog:
    def __init__(self, nc, n_dma_sems=12, strict=("act", "dve", "pool")):
        self.nc = nc
        self.ops = {e: [] for e in ENGS}
        self.lastw = {}
        self.readers = {}
        self.seen = {e: {} for e in ENGS}
        self.seen_dma = {e: set() for e in ENGS}
        self.strict = set(strict)
        self.n_dma_sems = n_dma_sems
        self.ndma = {e: 0 for e in ENGS}
        self.regions = {}
        self.conf = {}

    def region(self, key, lo, hi):
        self.regions[key] = (lo, hi)
        self.conf = {}

    def _conf(self, k):
        c = self.conf.get(k)
        if c is None:
            if k in self.regions:
                lo, hi = self.regions[k]
                c = [k2 for k2, (l2, h2) in self.regions.items() if l2 < hi and lo < h2]
            else:
                c = [k]
            self.conf[k] = c
        return c

    def op(self, eng, fn, reads=(), writes=(), dma=False, extra=()):
        idx = len(self.ops[eng])
        rec = dict(eng=eng, fn=fn, waits={}, dma_waits=[], signal=False, dma=dma, idx=idx)
        deps = set(extra)
        for k0 in reads:
            for k in self._conf(k0):
                w = self.lastw.get(k)
                if w is not None:
                    deps.add(w)
                if k.startswith("ps"):
                    deps.update(r for r in self.readers.get(k, ()) if r[0] != eng)
        for k0 in writes:
            for k in self._conf(k0):
                w = self.lastw.get(k)
                if w is not None:
                    deps.add(w)
                deps.update(self.readers.get(k, ()))
        for (e2, i2) in deps:
            r2 = self.ops[e2][i2]
            if r2["dma"]:
                if (e2, i2) in self.seen_dma[eng]:
                    continue
                self.seen_dma[eng].add((e2, i2))
                rec["dma_waits"].append((e2, i2))
            else:
                if e2 == eng and eng not in self.strict:
                    continue
                if i2 <= self.seen[eng].get(e2, -1):
                    continue
                rec["waits"][e2] = max(rec["waits"].get(e2, -1), i2)
        for e2, i2 in rec["waits"].items():
            self.seen[eng][e2] = i2
            self.ops[e2][i2]["signal"] = True
        if dma:
            rec["dma_j"] = self.ndma[eng]
            self.ndma[eng] += 1
        self.ops[eng].append(rec)
        for k in writes:
            self.lastw[k] = (eng, idx)
            self.readers[k] = []
        for k in reads:
            self.readers.setdefault(k, []).append((eng, idx))
        return (eng, idx)

    def fence(self, extra=()):
        last = []
        for e in ("pe", "act", "dve", "pool"):
            for i in range(len(self.ops[e]) - 1, -1, -1):
                r = self.ops[e][i]
                if not r["dma"] and r["fn"] is not None:
                    last.append((e, i))
                    break
        for e in ENGS:
            self.op(e, None, extra=[d for d in last if d[0] != e] + list(extra))

    def emit(self, final_waits=()):
        nc = self.nc
        with contextlib.ExitStack() as st:
            csem = {e: st.enter_context(nc.semaphore("c_" + e)) for e in ENGS}
            dsem = {e: [st.enter_context(nc.semaphore("d_%s_%d" % (e, i))) for i in range(self.n_dma_sems)]
                    for e in ENGS if self.ndma[e] > 0}
            for e in ENGS:
                c = 0
                for r in self.ops[e]:
                    if r["signal"] and not r["dma"]:
                        c += 1
                    r["cval"] = c
            N = self.n_dma_sems

            def dma_sem_val(e2, i2):
                j = self.ops[e2][i2]["dma_j"]
                return e2, j % N, 16 * (j // N + 1)

            block = st.enter_context(nc.Block())

            def run(e, eng):
                for r in self.ops[e]:
                    for e2, i2 in r["waits"].items():
                        eng.wait_ge(csem[e2], self.ops[e2][i2]["cval"])
                    dw = {}
                    for (e2, i2) in r["dma_waits"]:
                        q, s, v = dma_sem_val(e2, i2)
                        dw[(q, s)] = max(dw.get((q, s), 0), v)
                    for (q, s), v in dw.items():
                        eng.wait_ge(dsem[q][s], v)
                    if r["fn"] is None:
                        continue
                    if r["dma"]:
                        j = r["dma_j"]
                        if j >= N:
                            eng.wait_ge(dsem[e][j % N], 16 * (j // N))
                    ins = r["fn"](eng)
                    if r["dma"]:
                        ins.then_inc(dsem[e][r["dma_j"] % N], 16)
                    elif r["signal"]:
                        ins.then_inc(csem[e], 1)
                if e == "sp":
                    dw = {}
                    for (e2, i2) in final_waits:
                        q, s, v = dma_sem_val(e2, i2)
                        dw[(q, s)] = max(dw.get((q, s), 0), v)
                    for (q, s), v in dw.items():
                        eng.wait_ge(dsem[q][s], v)

            @block.tensor
            def _(eng):
                run("pe", eng)

            @block.scalar
            def _(eng):
                run("act", eng)

            @block.vector
            def _(eng):
                run("dve", eng)

            @block.gpsimd
            def _(eng):
                run("pool", eng)

            @block.sync
            def _(eng):
                run("sp", eng)


O_G1, O_G3, O_BG, O_CW, O_CB, O_LG, O_LB, O_FW, O_FB, O_IF, O_SG = 0, 8, 16, 32, 280, 288, 296, 304, 436, 480, 481
NPRM = 482
NROW = 2064


def build_program():
    nc = bass.Bass("TRN2", target_bir_lowering=False)
    x_d = nc.dram_tensor("x", [8192, 1024], F32, kind="ExternalInput").ap()
    pos_d = nc.dram_tensor("pos", [4, 2048], I32, kind="ExternalInput").ap()
    win_d = nc.dram_tensor("w_in_p", [1024, 5376], F32, kind="ExternalInput").ap()
    wb_d = nc.dram_tensor("w_b_p", [1024, 2048], F32, kind="ExternalInput").ap()
    wout_d = nc.dram_tensor("w_out", [1024, 1024], F32, kind="ExternalInput").ap()
    wup_d = nc.dram_tensor("w_up_p", [1024, 5632], F32, kind="ExternalInput").ap()
    wdn_d = nc.dram_tensor("w_down", [2816, 1024], F32, kind="ExternalInput").ap()
    prm_d = nc.dram_tensor("prm", [128, NPRM], F32, kind="ExternalInput").ap()
    rowp_d = nc.dram_tensor("rowp", [1, NROW], F32, kind="ExternalInput").ap()
    y_d = nc.dram_tensor("y", [8192, 1024], F32, kind="ExternalOutput").ap()
    wA = nc.dram_tensor("wA", [11, 128, 8, 512], BF16).ap()
    wB = nc.dram_tensor("wB", [4, 128, 8, 512], BF16).ap()
    wC = nc.dram_tensor("wC", [2, 128, 8, 512], BF16).ap()
    wD = nc.dram_tensor("wD", [11, 128, 8, 512], BF16).ap()
    wE = nc.dram_tensor("wE", [2, 128, 22, 512], BF16).ap()
    wF = nc.dram_tensor("wF", [8, 128, 4096], BF16).ap()
    wG = nc.dram_tensor("wG", [5, 128, 4096], BF16).ap()

    def sb(name, shape, dt):
        return nc.alloc_sbuf_tensor(name, shape, dt).ap()

    xbuf = sb("xbuf", [128, 2, 4, 1024], F32)
    wring = sb("wring", [128, NWSLOT, 8, 512], BF16)
    wflat = wring.rearrange("p s k t -> p s (k t)")
    hTm = sb("hTm", [128, 4, 1024], BF16)
    hTa = sb("hTa", [128, 8, 512], BF16)
    hTb = sb("hTb", [128, 8, 512], BF16)
    kT = [sb("kT0", [128, 640], BF16), sb("kT1", [128, 640], BF16)]
    V = sb("V", [128, 5, 2, 66], BF16)
    uhalo = sb("uhalo", [128, 8, 30], BF16)
    rawhalo = sb("rawhalo", [128, 44, 2], BF16)
    prm = sb("prm_sb", [128, NPRM], F32)
    rowp = sb("rowp_sb", [128, NROW], F32)
    esink = sb("esink", [128, 16], F32)
    ident = sb("ident", [128, 128], BF16)
    ones = sb("ones", [128, 128], BF16)
    maskP = sb("maskP", [128, 4, 128], BF16)
    maskC = sb("maskC", [128, 4, 128], BF16)
    junk = sb("junk", [128, 1024], BF16)
    st = sb("st", [128, 64], F32)
    mhalf = sb("mhalf", [128, 8], F32)
    hb = sb("hb", [128, 16], F32)
    arena = sb("arena", [128, 51440], BF16)
    ps = nc.alloc_psum_tensor("ps", [128, 8, 512], F32).ap()

    def sub(off, n):
        return arena[:, off:off + n]

    uT = sub(0, 4336).rearrange("p (c t) -> p c t", c=8)
    ucT = sub(4336, 4096).rearrange("p (c t) -> p c t", c=8)
    usq = sub(8432, 1024).rearrange("p (c t) -> p c t", c=2)
    qT = sub(9456, 4096).rearrange("p (c t) -> p c t", c=8)
    gcT = sub(13552, 4096).rearrange("p (c t) -> p c t", c=8)
    gaT = sub(17648, 4096).rearrange("p (c t) -> p c t", c=8)
    PT = sub(21744, 4096).rearrange("p (s k t) -> p s k t", s=2, k=2)
    attn_tm = sub(25840, 2048).rearrange("p (s t) -> p s t", s=2)
    attnT = sub(27888, 4096).rearrange("p (c t) -> p c t", c=8)
    mergedT = sub(31984, 4096).rearrange("p (c t) -> p c t", c=8)
    diag = sub(36080, 4096).rearrange("p (s k t) -> p s k t", s=2, k=16)
    Rm = sub(36080, 128)
    posi = sub(40176, 1024).bitcast(I32)
    cosT = sub(41200, 1024).bitcast(F32)
    sinT = sub(42224, 1024).bitcast(F32)
    rt = [sub(43248, 1024).bitcast(F32), sub(44272, 1024).bitcast(F32)]
    tmpA = sub(45296, 2048).bitcast(F32).rearrange("p (s t) -> p s t", s=2)
    tmpB = sub(47344, 2048).bitcast(F32).rearrange("p (s t) -> p s t", s=2)
    sig = sub(49392, 2048).bitcast(F32).rearrange("p (s t) -> p s t", s=2)
    wEs = sub(0, 22528).rearrange("p (h k t) -> p h k t", h=2, k=22)
    zT = sub(22528, 11264).rearrange("p (c t) -> p c t", c=22)
    raw = sub(33792, 2064).rearrange("p (s t) -> p s t", s=4)
    sg = sub(35856, 2048).bitcast(F32).rearrange("p (s t) -> p s t", s=2)
    fdring = sub(39440, 7680).rearrange("p (s m t) -> p s m t", s=2, m=30)
    stin = sub(0, 16384).bitcast(F32).rearrange("p (s t) -> p s t", s=4)
    stout = sub(16384, 8192).rearrange("p (s t) -> p s t", s=4)
    identf = sub(24576, 1024).bitcast(F32)

    P = Prog(nc)
    R = P.region
    R("uTh", 0, 4336)
    for c in range(8):
        R("uT%d" % c, c * 542, (c + 1) * 542)
        R("uc%d" % c, 4336 + c * 512, 4336 + (c + 1) * 512)
        R("qT%d" % c, 9456 + c * 512, 9456 + (c + 1) * 512)
        R("g%d" % c, 13552 + c * 512, 13552 + (c + 1) * 512)
        R("g%d" % (8 + c), 17648 + c * 512, 17648 + (c + 1) * 512)
        R("mg%d" % c, 31984 + c * 512, 31984 + (c + 1) * 512)
    for sl in range(2):
        R("usq%d" % sl, 8432 + sl * 512, 8432 + (sl + 1) * 512)
        for kb in range(2):
            for half in range(2):
                o = 21744 + sl * 2048 + kb * 1024 + half * 512
                R("PT%d_%d_%d" % (sl, kb, half), o, o + 512)
        for g in range(2):
            for b2 in range(2):
                o = 25840 + sl * 1024 + (g * 8 + 4 * b2) * 64
                R("atm%d_%d_%d" % (sl, g, b2), o, o + 256)
        R("tA%d" % sl, 45296 + sl * 1024, 45296 + (sl + 1) * 1024)
        R("tB%d" % sl, 47344 + sl * 1024, 47344 + (sl + 1) * 1024)
        R("sig%d" % sl, 49392 + sl * 1024, 49392 + (sl + 1) * 1024)
        R("rt%d" % sl, 43248 + sl * 1024, 43248 + (sl + 1) * 1024)
        R("wE%d" % sl, sl * 11264, (sl + 1) * 11264)
        R("sg%d" % sl, 35856 + sl * 1024, 35856 + (sl + 1) * 1024)
        R("fdr%d" % sl, 39440 + sl * 3840, 39440 + (sl + 1) * 3840)
    for n in range(4):
        R("aT%d" % n, 27888, 31984)
        R("rawh%d" % n, 33792 + n * 516, 33792 + n * 516 + 2)
        R("raw%d" % n, 33792 + n * 516 + 2, 33792 + (n + 1) * 516)
    R("fx0", 39440, 40464)
    R("fx1", 40464, 41488)
    R("Rm", 36080, 36208)
    R("posi", 40176, 41200)
    R("cosT", 41200, 42224)
    R("sinT", 42224, 43248)
    for j in range(22):
        R("z%d" % j, 22528 + j * 512, 22528 + (j + 1) * 512)
    R("identf", 24576, 25600)
    for sl in range(4):
        R("stin%d" % sl, sl * 4096, (sl + 1) * 4096)
        R("stout%d" % sl, 16384 + sl * 2048, 16384 + (sl + 1) * 2048)

    def DMA(q, out, in_, reads=(), writes=(), extra=()):
        return P.op(q, lambda e: e.dma_start(out=out, in_=in_), reads, writes, dma=True, extra=extra)

    def ACT(out, in_, func, reads, writes, scale=1.0, bias=0.0, accum=None):
        def fn(e):
            if accum is not None:
                return e.activation(out=out, in_=in_, func=func, scale=scale, bias=bias, accum_out=accum)
            return e.activation(out=out, in_=in_, func=func, scale=scale, bias=bias)
        return P.op("act", fn, reads, writes)

    def TS(eng, out, in0, s1, s2, op0, op1, reads, writes):
        def fn(e):
            if s2 is None:
                return e.tensor_scalar(out=out, in0=in0, scalar1=s1, scalar2=None, op0=op0)
            return e.tensor_scalar(out=out, in0=in0, scalar1=s1, scalar2=s2, op0=op0, op1=op1)
        return P.op(eng, fn, reads, writes)

    def STT(out, in0, scalar, in1, op0, op1, reads, writes):
        return P.op("dve", lambda e: e.scalar_tensor_tensor(out=out, in0=in0, scalar=scalar, in1=in1, op0=op0, op1=op1),
                    reads, writes)

    def TT(eng, out, in0, in1, op, reads, writes):
        return P.op(eng, lambda e: e.tensor_tensor(out=out, in0=in0, in1=in1, op=op), reads, writes)

    def CP(eng, out, in_, reads, writes):
        return P.op(eng, lambda e: e.tensor_copy(out=out, in_=in_), reads, writes)

    def MEMSET(eng, ap, val, writes):
        return P.op(eng, lambda e: e.memset(ap, val), (), writes)

    def MM(out, pairs, reads, writes, start=True, stop=True):
        def fn(e):
            n = len(pairs)
            ins = None
            for i, (l, r) in enumerate(pairs):
                ins = e.matmul(out, lhsT=l, rhs=r, start=(start and i == 0), stop=(stop and i == n - 1))
            return ins
        return P.op("pe", fn, reads, writes)

    def TRN(outs_ins, reads, writes):
        def fn(e):
            ins = None
            for (o, i_) in outs_ins:
                ins = e.transpose(out=o, in_=i_, identity=ident)
            return ins
        return P.op("pe", fn, reads, writes)

    bank_ctr = [0]

    def next_bank():
        b = bank_ctr[0] % 4
        bank_ctr[0] += 1
        return b

    wctr = [0]

    def wload(src):
        slot = wctr[0] % NWSLOT
        wctr[0] += 1
        w = src.shape[-1]
        DMA("sp", wring[:, slot, :, 0:w], src, (), ["wr%d" % slot])
        return slot

    def wload_flat(src):
        slot = wctr[0] % NWSLOT
        wctr[0] += 1
        DMA("sp", wflat[:, slot, 0:src.shape[-1]], src, (), ["wr%d" % slot])
        return slot

    def dgview(slot):
        return wflat[:, slot, :].rearrange("p (m t) -> p m t", m=32)

    def pcol(off, j=0):
        return prm[:, off + j:off + j + 1]

    DMA("sp", prm, prm_d, (), ["prm"])
    DMA("sp", rowp, rowp_d.partition_broadcast(128), (), ["rowp"])
    P.op("pool", lambda e: e.iota(identf, pattern=[[0, 4], [1, 128]], base=0, channel_multiplier=-1,
                                  allow_small_or_imprecise_dtypes=True), (), ["identf"])
    TS("dve", ident, identf[:, 0:128], 0.0, None, ALU.is_equal, None, ["identf"], ["ident"])
    NEG = -30000.0
    TS("dve", maskC.rearrange("p a b -> p (a b)"), identf, 0.0, NEG, ALU.is_lt, ALU.mult, ["identf"], ["maskC"])
    TS("dve", maskP.rearrange("p a b -> p (a b)"), identf, 0.0, NEG, ALU.is_ge, ALU.mult, ["identf"], ["maskP"])
    MEMSET("dve", ones, 1.0, ["ones"])
    MEMSET("dve", mhalf, -0.5, ["mhalf"])
    MEMSET("dve", kT[0], 0.0, ["kT"])
    MEMSET("dve", kT[1], 0.0, ["kT"])
    MEMSET("dve", V.rearrange("p a b c -> p (a b c)"), 1.0, ["V"])
    ACT(esink, rowp[:, 2048:2064], AF.Exp, ["rowp"], ["esink"])
    TS("dve", hb, prm[:, O_BG:O_BG + 16], 0.5, None, ALU.mult, None, ["prm"], ["hb"])
    TS("dve", rowp[:, 0:1024], rowp[:, 0:1024], 0.5, None, ALU.mult, None, ["rowp"], ["rowp"])

    stores = []
    pc = [0]

    def prep(src, dst, KC, C, gain_off):
        for kc in range(KC):
            c0 = 0
            while c0 < C:
                cw = min(2048, C - c0)
                if cw > 512 and cw % 512:
                    cw = (cw // 512) * 512
                slot = pc[0] % 4
                DMA("sp", stin[:, slot, 0:cw], src[kc * 128:(kc + 1) * 128, c0:c0 + cw], (), ["stin%d" % slot])
                eng = "act" if pc[0] % 2 == 0 else "dve"
                o_ap = stout[:, slot, 0:cw]
                i_ap = stin[:, slot, 0:cw]
                rd = ["stin%d" % slot, "prm"]
                wr = ["stout%d" % slot]
                if gain_off is None:
                    if eng == "act":
                        ACT(o_ap, i_ap, AF.Copy, rd, wr)
                    else:
                        CP("dve", o_ap, i_ap, rd, wr)
                else:
                    g = pcol(gain_off, kc)
                    if eng == "act":
                        ACT(o_ap, i_ap, AF.Identity, rd, wr, scale=g)
                    else:
                        TS("dve", o_ap, i_ap, g, None, ALU.mult, None, rd, wr)
                u0 = c0 // 512
                w = min(512, cw)
                nu = cw // w
                d_ap = dst[u0:u0 + nu, :, kc, 0:w].rearrange("u p c -> p u c")
                s_ap = stout[:, slot, 0:cw].rearrange("p (u c) -> p u c", u=nu)
                stores.append(DMA("act", d_ap, s_ap, ["stout%d" % slot], ()))
                pc[0] += 1
                c0 += cw

    stbf = sub(0, 16384).rearrange("p (s m t) -> p s m t", s=4, m=32)

    def build_diags(dst, cols):
        slot = pc[0] % 4
        pc[0] += 1

        def fn(e):
            ins = None
            for m, col in enumerate(cols):
                ins = e.tensor_scalar(out=stbf[:, slot, m, :], in0=ident, scalar1=prm[:, col:col + 1],
                                      scalar2=None, op0=ALU.mult)
            return ins
        P.op("dve", fn, ["ident", "prm"], ["stin%d" % slot])
        n = len(cols)
        stores.append(DMA("act", dst[:, 0:n * 128], stbf[:, slot, 0:n, :].rearrange("p m t -> p (m t)"),
                          ["stin%d" % slot], ()))

    prep(win_d, wA, 8, 5376, O_G1)
    for c in range(8):
        build_diags(wF[c], [O_CW + c * 31 + k for k in range(31)])
    prep(wb_d, wB, 8, 2048, None)
    prep(wout_d, wC, 8, 1024, None)
    for u in range(5):
        chs = range(10 * u, min(44, 10 * u + 10))
        build_diags(wG[u], [O_FW + ch * 3 + k for ch in chs for k in range(3)])
    prep(wup_d, wD, 8, 5632, O_G3)
    prep(wdn_d, wE, 22, 1024, None)

    def load_x(i):
        b = i % 2
        t0 = i * 512
        DMA("sp", xbuf[:, b, :, :], x_d[t0:t0 + 512, :].rearrange("(s p) d -> p s d", p=128),
            (), ["xb%ds%d" % (b, s) for s in range(4)])

    def pre_stats(b, s):
        ACT(junk, xbuf[:, b, s, :], AF.Square, ["xb%ds%d" % (b, s)], ["junk", "ssq%d" % s], accum=st[:, s:s + 1])
        TS("dve", st[:, 4 + s:5 + s], st[:, s:s + 1], 1.0 / 1024, 1e-6, ALU.mult, ALU.add, ["ssq%d" % s], ["ms%d" % s])
        TT("pool", st[:, 8 + s:9 + s], st[:, 4 + s:5 + s], mhalf[:, 0:1], ALU.pow, ["ms%d" % s, "mhalf"], ["rstd%d" % s])

    def pre_h(b, s):
        TS("dve", hTm[:, s, :], xbuf[:, b, s, :], st[:, 8 + s:9 + s], None, ALU.mult, None,
           ["xb%ds%d" % (b, s), "rstd%d" % s], ["hTm%d" % s])

    def pre_TR(s, which):
        hT = hTa if which == "a" else hTb
        tb = 4 if s % 2 == 0 else 7
        pst = ps[:, tb, :].bitcast(BF16)
        TRN([(pst[:, kc * 128:(kc + 1) * 128], hTm[:, s, kc * 128:(kc + 1) * 128]) for kc in range(8)],
            ["hTm%d" % s, "ident"], ["ps%d" % tb])
        if s % 2:
            CP("dve", hT[:, :, s * 128:(s + 1) * 128], pst.rearrange("p (k t) -> p k t", k=8), ["ps%d" % tb],
               ["hT%s%d" % (which, s)])
        else:
            ACT(hT[:, :, s * 128:(s + 1) * 128], pst.rearrange("p (k t) -> p k t", k=8), AF.Copy, ["ps%d" % tb],
                ["hT%s%d" % (which, s)])

    def pre_T(b, s, which):
        pre_h(b, s)
        pre_TR(s, which)

    HTA = ["hTa%d" % s for s in range(4)]
    HTB = ["hTb%d" % s for s in range(4)]

    def rope_tables(i):
        seq, ti = i // 4, i % 4
        DMA("sp", posi, pos_d[seq:seq + 1, ti * 512:(ti + 1) * 512].partition_broadcast(128), (), ["posi"])
        ang = tmpA[:, 0, :]
        a2 = tmpA[:, 1, :]
        kf = tmpB[:, 0, :]
        ki = tmpB[:, 1, :].bitcast(I32)
        C1 = 6.28125
        C2 = float(2 * np.pi - 6.28125)
        CL = 3.1415925
        CP("dve", ang, posi, ["posi"], ["tA0"])
        TS("dve", ang, ang, pcol(O_IF), None, ALU.mult, None, ["tA0", "prm"], ["tA0"])
        for (tab, shift, key) in ((sinT, 0.0, "sinT"), (cosT, PI / 2, "cosT")):
            TS("dve", a2, ang, shift, None, ALU.add, None, ["tA0"], ["tA1"])
            TS("dve", kf, a2, float(1.0 / (2 * np.pi)), None, ALU.mult, None, ["tA1"], ["tB0"])
            CP("dve", ki, kf, ["tB0"], ["tB1"])
            CP("dve", kf, ki, ["tB1"], ["tB0"])
            STT(a2, kf, -C1, a2, ALU.mult, ALU.add, ["tB0", "tA1"], ["tA1"])
            STT(a2, kf, -C2, a2, ALU.mult, ALU.add, ["tB0", "tA1"], ["tA1"])
            TS("dve", a2, a2, CL, -CL, ALU.min, ALU.max, ["tA1"], ["tA1"])
            if key == "sinT":
                ACT(tab, a2, AF.Sin, ["tA1", "prm"], [key], scale=pcol(O_SG))
            else:
                ACT(tab, a2, AF.Sin, ["tA1"], [key])

    def proj_chunk(slot, j, which="a"):
        b = next_bank()
        hT = hTa if which == "a" else hTb
        MM(ps[:, b, :], [(wring[:, slot, kc, j * 128:(j + 1) * 128], hT[:, kc, :]) for kc in range(8)],
           ["wr%d" % slot] + (HTA if which == "a" else HTB), ["ps%d" % b])
        return b

    def phase_A1(i):
        first = (i % 4 == 0)
        if not first:
            for g in range(2):
                CP("pool", kT[g][:, 0:128], kT[g][:, 512:640], ["kT"], ["kT"])
            CP("pool", V[:, 0, :, 0:64], V[:, 4, :, 0:64], ["V"], ["V"])
            CP("pool", uT[:, :, 0:30], uhalo, ["uhalo"], ["uTh"])
        else:
            MEMSET("pool", uT[:, :, 0:30], 0.0, ["uTh"])
        for u in range(4):
            slot = wload(wA[u])
            for half in range(2):
                c = 2 * u + half
                bv = proj_chunk(slot, 2 * half)
                bg = proj_chunk(slot, 2 * half + 1)
                sl = c % 2
                ACT(sig[:, sl, :], ps[:, bg, :], AF.Tanh, ["ps%d" % bg], ["sig%d" % sl], scale=0.5)
                STT(uT[:, c, 30:542], sig[:, sl, :], 1.0, ps[:, bv, :], ALU.add, ALU.mult,
                    ["ps%d" % bv, "sig%d" % sl], ["uT%d" % c])

    def phase_A2(i):
        for (d0, s0) in ((0, 32), (32, 0), (64, 96), (96, 64)):
            CP("dve", Rm[:, d0:d0 + 32], ident[:, s0:s0 + 32], ["ident"], ["Rm"])

        for u in range(4):
            slot = wload(wA[6 + u])
            for j in range(4):
                cc = 4 * u + j
                b = proj_chunk(slot, j)
                dst = gcT[:, cc, :] if cc < 8 else gaT[:, cc - 8, :]
                ACT(dst, ps[:, b, :], AF.Tanh, ["ps%d" % b, "hb"], ["g%d" % cc], scale=0.5, bias=hb[:, cc:cc + 1])

        def rope_a(bq, sl):
            ACT(usq[:, sl, :], ps[:, bq, :], AF.Copy, ["ps%d" % bq], ["usq%d" % sl])

        def rope_b(bq, sl, r0, r1, k0, k1, fin):
            br = next_bank()
            MM(ps[:, br, :], [(Rm, usq[:, sl, :])], ["Rm", "usq%d" % sl], ["ps%d" % br])
            TT("dve", r0, ps[:, bq, :], cosT, ALU.mult, ["ps%d" % bq, "cosT"], [k0])
            TT("dve", r1, ps[:, br, :], sinT, ALU.mult, ["ps%d" % br, "sinT"], [k1])
            fin()

        pend = [None]

        def flush():
            if pend[0] is not None:
                rope_b(*pend[0])
                pend[0] = None

        for u in range(2):
            slot = wload(wA[4 + u])
            for j in range(4):
                c = 4 * u + j
                bq = proj_chunk(slot, j)
                rope_a(bq, c % 2)
                flush()
                if c % 2 == 0:
                    r0, r1, k0, k1 = rt[0], rt[1], "rt0", "rt1"
                else:
                    r0, r1, k0, k1 = sig[:, 0, :], sig[:, 1, :], "sig0", "sig1"

                def fin(c=c, r0=r0, r1=r1, k0=k0, k1=k1):
                    TT("pool", qT[:, c, :], r0, r1, ALU.add, [k0, k1], ["qT%d" % c])
                    ln_apply(c)
                pend[0] = (bq, c % 2, r0, r1, k0, k1, fin)
        slot = wload(wA[10][:, :, 0:256])
        bq = proj_chunk(slot, 0)
        rope_a(bq, 0)
        flush()

        def fink():
            TT("pool", kT[0][0:64, 128:640], rt[0][0:64, :], rt[1][0:64, :], ALU.add, ["rt0", "rt1"], ["kT"])
            TT("pool", kT[1][64:128, 128:640], rt[0][64:128, :], rt[1][64:128, :], ALU.add, ["rt0", "rt1"], ["kT"])
        pend[0] = (bq, 0, rt[0], rt[1], "rt0", "rt1", fink)
        for s in range(4):
            MM(ps[:, 7, s * 128:(s + 1) * 128],
               [(hTa[:, kc, s * 128:(s + 1) * 128], wring[:, slot, kc, 128:256]) for kc in range(8)],
               ["wr%d" % slot] + HTA, ["ps7"])
        ACT(V[:, 1:5, :, 0:64], ps[:, 7, :].rearrange("p (s g d) -> p s g d", s=4, g=2), AF.Copy, ["ps7"], ["V"])
        flush()

    dctr = [0]

    def phase_B(i):
        CP("pool", uhalo, uT[:, :, 512:542], ["uT%d" % c for c in range(8)], ["uhalo"])

        def stats(c):
            sl = c % 2
            MM(ps[:, 5, :], [(ones, ucT[:, c, :])], ["ones", "uc%d" % c], ["ps5"], start=(c == 0), stop=(c == 7))
            MM(ps[:, 6, :], [(ones, usq[:, sl, :])], ["ones", "usq%d" % sl], ["ps6"], start=(c == 0), stop=(c == 7))

        for c in range(8):
            b = next_bank()
            dslot = wload_flat(wF[c][:, 0:31 * 128])
            dv = dgview(dslot)
            MM(ps[:, b, :], [(dv[:, k, :], uT[:, c, k:k + 512]) for k in range(31)],
               ["wr%d" % dslot, "uT%d" % c, "uTh"], ["ps%d" % b])
            sl = c % 2
            ACT(ucT[:, c, :], ps[:, b, :], AF.Identity, ["ps%d" % b, "prm"], ["uc%d" % c], scale=0.5, bias=pcol(O_CB, c))
            ACT(usq[:, sl, :], ps[:, b, :], AF.Square, ["ps%d" % b, "prm"], ["usq%d" % sl], scale=0.5, bias=pcol(O_CB, c))
            if c >= 1:
                stats(c - 1)
        stats(7)

    def ln_head():
        mean = tmpA[:, 0, :]
        m2 = tmpA[:, 1, :]
        var = tmpB[:, 0, :]
        TS("dve", mean, ps[:, 5, :], 1.0 / 1024, None, ALU.mult, None, ["ps5"], ["tA0"])
        TT("dve", m2, mean, mean, ALU.mult, ["tA0"], ["tA1"])
        STT(var, ps[:, 6, :], 1.0 / 1024, m2, ALU.mult, ALU.subtract, ["ps6", "tA1"], ["tB0"])
        TS("dve", var, var, 1e-5, None, ALU.add, None, ["tB0"], ["tB0"])
        ACT(var, var, AF.Sqrt, ["tB0"], ["tB0"])
        P.op("dve", lambda e: e.reciprocal(out=ps[:, 5, :], in_=var), ["tB0"], ["ps5"])
        STT(ps[:, 6, :], mean, -1.0, ps[:, 5, :], ALU.mult, ALU.mult, ["tA0", "ps5"], ["ps6"])

    def ln_apply(c):
        sl = c % 2
        tk = "tB1" if sl else "tA1"
        tb = tmpB[:, 1, :] if sl else tmpA[:, 1, :]
        TT("dve", tb, ucT[:, c, :], ps[:, 5, :], ALU.mult, ["uc%d" % c, "ps5"], [tk])
        TT("dve", tb, tb, ps[:, 6, :], ALU.add, [tk, "ps6"], [tk])
        ACT(ucT[:, c, :], tb, AF.Silu, [tk, "prm"], ["uc%d" % c], scale=pcol(O_LG, c), bias=pcol(O_LB, c))

    sbank = [0]

    def phase_C(i):
        first = (i % 4 == 0)
        OB = (3, 7)

        def unit_S(k):
            n, g = k // 2, k % 2
            has_prev = not (first and n == 0)
            kbs = ([0] if has_prev else []) + [1]
            psl = k % 2
            for kb in kbs:
                kcols = slice(128 * (n + kb), 128 * (n + kb) + 128)
                msk = maskP if kb == 0 else maskC
                for half in range(2):
                    b = sbank[0] % 3
                    sbank[0] += 1
                    MM(ps[:, b, :], [(kT[g][:, kcols], qT[:, 4 * half:4 * half + 4, 128 * n:128 * n + 128]),
                                     (ident, msk)],
                       ["kT", "ident", "maskP", "maskC"] + ["qT%d" % c for c in range(4 * half, 4 * half + 4)],
                       ["ps%d" % b])
                    ACT(PT[:, psl, kb, half * 512:(half + 1) * 512], ps[:, b, :], AF.Exp, ["ps%d" % b],
                        ["PT%d_%d_%d" % (psl, kb, half)], scale=0.125)

        def unit_PV(k):
            n, g = k // 2, k % 2
            has_prev = not (first and n == 0)
            kbs = ([0] if has_prev else []) + [1]
            psl = k % 2
            asl = n % 2
            for bank2 in range(2):
                ob = OB[bank2]
                for cc in range(4):
                    c = 4 * bank2 + cc
                    MM(ps[:, ob, cc * 65:cc * 65 + 65],
                       [(PT[:, psl, kb, c * 128:(c + 1) * 128], V[:, n + kb, g, 0:65]) for kb in kbs],
                       ["V"] + ["PT%d_%d_%d" % (psl, kb, bank2) for kb in kbs], ["ps%d" % ob])
                h0 = g * 8 + 4 * bank2
                o3 = ps[:, ob, 0:260].rearrange("p (c d) -> p c d", c=4)
                den = st[:, 16 + 4 * bank2:20 + 4 * bank2]
                TT("dve", den, o3[:, :, 64], esink[:, h0:h0 + 4], ALU.add, ["ps%d" % ob, "esink"], ["den%d" % bank2])
                P.op("dve", lambda e, den=den: e.reciprocal(out=den, in_=den), ["den%d" % bank2], ["den%d" % bank2])

                def nfn(e, bank2=bank2, h0=h0, o3=o3, asl=asl):
                    ins = None
                    for cc in range(4):
                        h = h0 + cc
                        ins = e.tensor_scalar(out=attn_tm[:, asl, h * 64:(h + 1) * 64], in0=o3[:, cc, 0:64],
                                              scalar1=st[:, 16 + 4 * bank2 + cc:17 + 4 * bank2 + cc],
                                              scalar2=None, op0=ALU.mult)
                    return ins
                P.op("dve", nfn, ["ps%d" % ob, "den%d" % bank2], ["atm%d_%d_%d" % (asl, g, bank2)])

        def unit_T(n):
            asl = n % 2
            pst = ps[:, 4, :].bitcast(BF16)
            TRN([(pst[:, kc * 128:(kc + 1) * 128], attn_tm[:, asl, kc * 128:(kc + 1) * 128]) for kc in range(8)],
                ["atm%d_%d_%d" % (asl, g, b2) for g in range(2) for b2 in range(2)] + ["ident"], ["ps4"])
            ACT(attnT[:, :, n * 128:(n + 1) * 128], pst.rearrange("p (k t) -> p k t", k=8), AF.Copy, ["ps4"], ["aT%d" % n])

        unit_S(0)
        for k in range(8):
            if k + 1 < 8:
                unit_S(k + 1)
            unit_PV(k)
            if k >= 2 and k % 2 == 0:
                unit_T(k // 2 - 1)
        unit_T(3)

    def post_norm(i, wslots_or_views, nk, lhs_buf, lhs_keys, goff, tms, hook_a=None, hook_pe=None):
        b = i % 2
        for s in range(4):
            mine = tms[2 * (s % 2):2 * (s % 2) + 2]
            for half in range(2):
                bk = next_bank()
                tm, tk = mine[half]
                MM(ps[:, bk, :], [(lhs_buf[:, kc, s * 128:(s + 1) * 128], wslots_or_views[half][0][:, kc, :]) for kc in range(nk)],
                   lhs_keys + [wslots_or_views[half][1]], ["ps%d" % bk])
                ACT(junk[:, 0:512], ps[:, bk, :], AF.Square, ["ps%d" % bk], ["junk", "pq%d" % half],
                    accum=st[:, 24 + half:25 + half])
                TT("dve", tm, ps[:, bk, :], rowp[:, goff + half * 512:goff + (half + 1) * 512], ALU.mult,
                   ["ps%d" % bk, "rowp"], [tk])
            TT("dve", st[:, 26:27], st[:, 24:25], st[:, 25:26], ALU.add, ["pq0", "pq1"], ["pms"])
            TS("dve", st[:, 26:27], st[:, 26:27], (1.0 / 4096) if goff == 0 else (1.0 / 1024), 1e-6, ALU.mult, ALU.add,
               ["pms"], ["pms"])
            TT("pool", st[:, 28 + s:29 + s], st[:, 26:27], mhalf[:, 0:1], ALU.pow, ["pms", "mhalf"], ["prs%d" % s])
            for half in range(2):
                tm, tk = mine[half]
                xk = "xb%ds%d" % (b, s)
                STT(xbuf[:, b, s, half * 512:(half + 1) * 512], tm, st[:, 28 + s:29 + s],
                    xbuf[:, b, s, half * 512:(half + 1) * 512], ALU.mult, ALU.add, [tk, xk, "prs%d" % s], [xk])
            if hook_a is not None:
                hook_a(s)
            if hook_pe is not None and s >= 1:
                hook_pe[0](s - 1)
        if hook_pe is not None:
            hook_pe[0](3)
            for s in range(4):
                hook_pe[1](s)

    def phase_D(i):
        for u in range(4):
            slot = wload(wB[u])
            for half in range(2):
                c = 2 * u + half
                ba = next_bank()
                MM(ps[:, ba, :], [(wring[:, slot, kc, (2 * half) * 128:(2 * half + 1) * 128], ucT[:, kc, :]) for kc in range(8)],
                   ["wr%d" % slot] + ["uc%d" % k for k in range(8)], ["ps%d" % ba])
                bb = next_bank()
                MM(ps[:, bb, :], [(wring[:, slot, kc, (2 * half + 1) * 128:(2 * half + 2) * 128], attnT[:, kc, :]) for kc in range(8)],
                   ["wr%d" % slot] + ["aT%d" % n for n in range(4)], ["ps%d" % bb])
                sl = c % 2
                STT(tmpA[:, sl, :], gcT[:, c, :], 1.0, ps[:, ba, :], ALU.add, ALU.mult, ["ps%d" % ba, "g%d" % c], ["tA%d" % sl])
                STT(tmpB[:, sl, :], gaT[:, c, :], 1.0, ps[:, bb, :], ALU.add, ALU.mult, ["ps%d" % bb, "g%d" % (8 + c)], ["tB%d" % sl])
                TT("pool", mergedT[:, c, :], tmpA[:, sl, :], tmpB[:, sl, :], ALU.add, ["tA%d" % sl, "tB%d" % sl], ["mg%d" % c])
        s0 = wload(wC[0])
        s1 = wload(wC[1])
        b = i % 2
        post_norm(i, [(wring[:, s0, :, :], "wr%d" % s0), (wring[:, s1, :, :], "wr%d" % s1)], 8, mergedT,
                  ["mg%d" % c for c in range(8)], 0,
                  [(tmpA[:, 0, :], "tA0"), (tmpA[:, 1, :], "tA1"), (tmpB[:, 0, :], "tB0"), (tmpB[:, 1, :], "tB1")],
                  hook_a=lambda s: pre_stats(b, s), hook_pe=(lambda s: pre_h(b, s), lambda s: pre_TR(s, "b")))

    rctr = [0]

    def phase_E(i):
        first = (i % 4 == 0)
        b = i % 2
        if i + 1 < NT:
            load_x(i + 1)
        pend = [None]

        def conv_part(rs, ch, which, j2, sgl, fslot):
            bc = next_bank()
            MM(ps[:, bc, :], [(fdring[:, fslot, (ch % 10) * 3 + k, :], raw[:, rs, k:k + 512]) for k in range(3)],
               ["raw%d" % rs, "rawh%d" % rs, "fdr%d" % fslot], ["ps%d" % bc])
            if which == 0:
                ACT(sg[:, sgl, :], ps[:, bc, :], AF.Silu, ["ps%d" % bc, "prm"], ["sg%d" % sgl], bias=pcol(O_FB, ch))
            else:
                STT(zT[:, j2, :], ps[:, bc, :], pcol(O_FB, ch), sg[:, sgl, :], ALU.add, ALU.mult,
                    ["ps%d" % bc, "sg%d" % sgl, "prm"], ["z%d" % j2])

        def fload(uu):
            nch = min(44, 10 * uu + 10) - 10 * uu
            fs = uu % 2
            DMA("sp", fdring[:, fs, 0:nch * 3, :], wG[uu][:, 0:nch * 3 * 128].rearrange("p (m t) -> p m t", t=128),
                (), ["fdr%d" % fs])
            return fs

        fsl = [fload(0)]
        fnext = [None]
        for u in range(11):
            slot = wload(wD[u])
            if u == 2:
                for hh in range(2):
                    DMA("sp", wEs[:, hh, :, :], wE[hh], (), ["wE%d" % hh])
            if u == 4 and i + 1 < NT:
                for s in range(4):
                    pre_stats((i + 1) % 2, s)
            if u in (5, 6, 7, 8) and i + 1 < NT:
                pre_T((i + 1) % 2, u - 5, "a")
            for half in range(2):
                j2 = 2 * u + half
                sgl = j2 % 2
                for which in range(2):
                    ch = 2 * j2 + which
                    bk = proj_chunk(slot, 2 * half + which, "b")
                    rs = rctr[0] % 4
                    rctr[0] += 1
                    ACT(raw[:, rs, 2:514], ps[:, bk, :], AF.Copy, ["ps%d" % bk], ["raw%d" % rs])

                    if ch % 10 == 0 and ch > 0:
                        fsl[0] = fnext[0]
                    if (ch + 4) % 10 == 0 and ch + 4 < 44:
                        fnext[0] = fload((ch + 4) // 10)
                    if first:
                        MEMSET("pool", raw[:, rs, 0:2], 0.0, ["rawh%d" % rs])
                    else:
                        CP("pool", raw[:, rs, 0:2], rawhalo[:, ch, :], ["rh%d" % ch], ["rawh%d" % rs])
                    CP("pool", rawhalo[:, ch, :], raw[:, rs, 512:514], ["raw%d" % rs], ["rh%d" % ch])
                    if pend[0] is not None:
                        conv_part(*pend[0])
                    pend[0] = (rs, ch, which, j2, sgl, fsl[0])
        conv_part(*pend[0])
        t0 = i * 512
        outs = []

        def store(s):
            outs.append(DMA("act", y_d[t0 + s * 128:t0 + (s + 1) * 128, :], xbuf[:, b, s, :], ["xb%ds%d" % (b, s)], ()))
        post_norm(i, [(wEs[:, 0, :, :], "wE0"), (wEs[:, 1, :, :], "wE1")], 22, zT,
                  ["z%d" % j for j in range(22)], 1024,
                  [(sg[:, 0, :], "sg0"), (sg[:, 1, :], "sg1"),
                   (sub(39440, 1024).bitcast(F32), "fx0"), (sub(40464, 1024).bitcast(F32), "fx1")],
                  hook_a=store)
        return outs

    load_x(0)
    P.fence(extra=stores)
    final = []
    for s in range(4):
        pre_stats(0, s)
    for s in range(4):
        pre_T(0, s, "a")
    for i in range(NT):
        phase_A1(i)
        rope_tables(i)
        phase_B(i)
        ln_head()
        phase_A2(i)
        phase_C(i)
        phase_D(i)
        final += phase_E(i)
    P.emit(final_waits=final)
    return nc


def _host_layout(inputs):
    f = np.float32
    w_in = np.asarray(inputs["w_in"], f)[0]
    cols = []
    for c in range(8):
        cols += list(range(c * 128, (c + 1) * 128)) + list(range(1024 + c * 128, 1024 + (c + 1) * 128))
    d = np.arange(64)
    for c in range(8):
        for h in (c, 8 + c):
            cols += list(2048 + h * 64 + d)
    cols += list(range(3328, 5376))
    cols += list(range(3072, 3200))
    cols += list(range(3200, 3328))
    w_in_p = np.ascontiguousarray(w_in[:, np.array(cols)])
    wco = np.asarray(inputs["w_conv_out"], f)[0]
    wao = np.asarray(inputs["w_attn_out"], f)[0]
    w_b_p = np.ascontiguousarray(np.concatenate(
        [m[:, c * 128:(c + 1) * 128] for c in range(8) for m in (wco, wao)], axis=1))
    w_up = np.asarray(inputs["w_up"], f)[0]
    upcols = []
    for j in range(22):
        upcols += list(range(j * 128, (j + 1) * 128)) + list(range(2816 + j * 128, 2816 + (j + 1) * 128))
    upcols = np.array(upcols)
    w_up_p = np.ascontiguousarray(w_up[:, upcols])
    prm = np.zeros((128, NPRM), f)
    prm[:, O_G1:O_G1 + 8] = np.asarray(inputs["ln_mix_pre"], f)[0].reshape(8, 128).T
    prm[:, O_G3:O_G3 + 8] = np.asarray(inputs["ln_ffn_pre"], f)[0].reshape(8, 128).T
    prm[:, O_BG:O_BG + 16] = np.asarray(inputs["b_gate"], f)[0].reshape(16, 128).T
    cw = np.asarray(inputs["conv_dw_w"], f)[0]
    prm[:, O_CW:O_CW + 248] = cw.T.reshape(8, 128, 31).transpose(1, 0, 2).reshape(128, 248)
    prm[:, O_CB:O_CB + 8] = np.asarray(inputs["conv_dw_b"], f)[0].reshape(8, 128).T
    prm[:, O_LG:O_LG + 8] = np.asarray(inputs["conv_ln_g"], f)[0].reshape(8, 128).T
    prm[:, O_LB:O_LB + 8] = np.asarray(inputs["conv_ln_b"], f)[0].reshape(8, 128).T
    fw = np.asarray(inputs["ffn_dw_w"], f)[0][:, upcols]
    prm[:, O_FW:O_FW + 132] = fw.T.reshape(44, 128, 3).transpose(1, 0, 2).reshape(128, 132)
    prm[:, O_FB:O_FB + 44] = np.asarray(inputs["ffn_dw_b"], f)[0][upcols].reshape(44, 128).T
    p = np.arange(128)
    inv_freq = (10000.0 ** (-(np.arange(0, 64, 2, dtype=np.float32)) / np.float32(64))).astype(f)
    prm[:, O_IF] = inv_freq[p % 32]
    prm[:, O_SG] = np.where((p % 64) < 32, -1.0, 1.0)
    rowp = np.zeros((1, NROW), f)
    rowp[0, 0:1024] = np.asarray(inputs["ln_mix_post"], f)[0]
    rowp[0, 1024:2048] = np.asarray(inputs["ln_ffn_post"], f)[0]
    rowp[0, 2048:2064] = np.asarray(inputs["attn_sinks"], f)[0]
    shared = {
        "w_in_p": w_in_p, "w_b_p": w_b_p, "w_out": np.ascontiguousarray(np.asarray(inputs["w_out"], f)[0]),
        "w_up_p": w_up_p, "w_down": np.ascontiguousarray(np.asarray(inputs["w_down"], f)[0]),
        "prm": prm, "rowp": rowp,
    }
    x = np.asarray(inputs["x"], f)
    pos = np.asarray(inputs["positions"], np.int32)
    in_maps = []
    for c in range(NCORES):
        m = dict(shared)
        m["x"] = np.ascontiguousarray(x[4 * c:4 * c + 4].reshape(8192, 1024))
        m["pos"] = np.ascontiguousarray(pos[4 * c:4 * c + 4])
        in_maps.append(m)
    return in_maps


_NC_CACHE = {}


def kernel(**inputs):
    in_maps = _host_layout(inputs)
    if "nc" not in _NC_CACHE:
        _NC_CACHE["nc"] = build_program()
    nc = _NC_CACHE["nc"]
    res = run_bass_kernel_spmd(nc, in_maps, core_ids=list(range(NCORES)))
    out = np.concatenate([np.asarray(r["y"]).reshape(4, 2048, 1024) for r in res.results], axis=0)
    return out.astype(np.float32)
```
